# Optimizing a Trainium2 kernel written in Bass

```python
import math
import jax, jax.numpy as jnp
from jax import lax
import numpy as np


D_MODEL = 1024
BATCH = 4
SEQ = 4096
DEPTH = 4
DEC_BATCH = 8
DEC_SEQ = 2048
PAST_LEN = 128

N_MEM = 256
GRID_W = 64
EPS = 1e-6
HY_WIDTH = 512
HY_SHORT = 3
HY_EMB = 33
HY_FH = 64
HY_SIN_FREQ = 1.0
HY_TARGET = 1e-2
HY_FAST = 0.3
HY_SLOW = 1.5
HY_SHIFT = 0.0
RET_HEADS = 4
RET_DK = 128
RET_DV = 128
RET_WIDTH = RET_HEADS * RET_DV
RET_CHUNK = 128
RET_THETA = 10000.0
ATT_HEADS = 4
ATT_KV_HEADS = 2
ATT_HD = 128
ATT_WIDTH = ATT_HEADS * ATT_HD
ATT_BLOCK = 128
ROPE_THETA = 10000.0
X_HEADS = 4
X_HD = D_MODEL // X_HEADS
D_FF = 4 * D_MODEL
N_BRANCH = 3
MIX_WIDTH = HY_WIDTH + RET_WIDTH + ATT_WIDTH
IN_SPLITS = (3 * HY_WIDTH, RET_HEADS * RET_DK, RET_HEADS * RET_DK, RET_WIDTH, RET_WIDTH, ATT_WIDTH, ATT_KV_HEADS * ATT_HD, ATT_KV_HEADS * ATT_HD, N_BRANCH * D_MODEL)
IN_COLS = sum(IN_SPLITS)

kernel_name = 'hybrid_hyena_retnet_gqa_encoder'


def _split_points(sizes):
    pts, acc = [], 0
    for s in sizes[:-1]:
        acc += s
        pts.append(acc)
    return pts


def rmsnorm(x, g):
    xf = x.astype(jnp.float32)
    y = xf * lax.rsqrt(jnp.mean(xf * xf, axis=-1, keepdims=True) + EPS)
    return (y * g.astype(jnp.float32)).astype(x.dtype)


def rotary_tables(pos, dim, theta):
    inv = theta ** (-jnp.arange(0, dim, 2, dtype=jnp.float32) / dim)
    ang = pos.astype(jnp.float32)[:, None] * inv[None, :]
    return jnp.cos(ang), jnp.sin(ang)


def apply_rotary(x, cos, sin):
    half = x.shape[-1] // 2
    c = cos[None, :, None, :].astype(x.dtype)
    s = sin[None, :, None, :].astype(x.dtype)
    x1, x2 = x[..., :half], x[..., half:]
    return jnp.concatenate([x1 * c - x2 * s, x1 * s + x2 * c], axis=-1)


def short_conv(u, w):
    L = u.shape[1]
    pad = HY_SHORT // 2
    up = jnp.pad(u, ((0, 0), (pad, pad), (0, 0)))
    out = up[:, 0:L] * w[0]
    for j in range(1, HY_SHORT):
        out = out + up[:, j:j + L] * w[j]
    return out


def hyena_spectrum(L, w1, b1, w2, b2, w3):
    f32 = jnp.float32
    t = jnp.linspace(0.0, 1.0, L, dtype=f32)[:, None]
    bands = (HY_EMB - 1) // 2
    w = 2.0 * math.pi * jnp.arange(L, dtype=f32)[:, None] / L
    f = jnp.linspace(1e-4, bands - 1, bands, dtype=f32)[None, :]
    z = jnp.concatenate([t, jnp.cos(f * w), -jnp.sin(f * w)], axis=-1)
    hdn = jnp.sin(HY_SIN_FREQ * (z @ w1.astype(f32) + b1.astype(f32)))
    hdn = jnp.sin(HY_SIN_FREQ * (hdn @ w2.astype(f32) + b2.astype(f32)))
    filt = (hdn @ w3.astype(f32)).reshape(L, 2, HY_WIDTH)
    min_decay = math.log(HY_TARGET) / HY_SLOW
    max_decay = math.log(HY_TARGET) / HY_FAST
    deltas = jnp.abs(jnp.linspace(min_decay, max_decay, HY_WIDTH, dtype=f32))
    mod = jnp.exp(-t * deltas[None, :]) + HY_SHIFT
    filt = filt * mod[:, None, :]
    k_fwd, k_bwd = filt[:, 0], filt[:, 1]
    kfull = jnp.concatenate([k_fwd, jnp.zeros((1, HY_WIDTH), f32), jnp.flip(k_bwd[1:], axis=0)], axis=0)
    return jnp.fft.rfft(kfull, axis=0)


def hyena_mixer(hy_in, conv_w, K, bias):
    u = short_conv(hy_in, conv_w)
    x0, x1, v = jnp.split(u, 3, axis=-1)
    L = u.shape[1]
    zf = (v * x1).astype(jnp.float32)
    Z = jnp.fft.rfft(zf, n=2 * L, axis=1)
    y = jnp.fft.irfft(Z * K[None], n=2 * L, axis=1)[:, :L] + zf * bias.astype(jnp.float32)
    return y.astype(x0.dtype) * x0


def retention_dir(q, k, v, log_gamma, strict):
    B, L, H, dk = q.shape
    dv = v.shape[-1]
    C = RET_CHUNK
    N = L // C
    qc = q.reshape(B, N, C, H, dk)
    kc = k.reshape(B, N, C, H, dk)
    vc = v.reshape(B, N, C, H, dv)
    idx = jnp.arange(C, dtype=jnp.float32)
    diff = idx[:, None] - idx[None, :]
    mask = diff > 0 if strict else diff >= 0
    dmat = jnp.where(mask[None], jnp.exp(jnp.maximum(diff, 0.0)[None] * log_gamma[:, None, None]), 0.0)
    scores = jnp.einsum('bnihd,bnjhd->bnhij', qc, kc) * dmat[None, None]
    intra = jnp.einsum('bnhij,bnjhe->bnihe', scores, vc)
    k_w = jnp.exp((C - 1 - idx)[:, None] * log_gamma[None, :])
    chunk_kv = jnp.einsum('bnjhd,jh,bnjhe->bnhde', kc, k_w, vc)
    chunk_decay = jnp.exp(C * log_gamma)[None, :, None, None]

    def step(S, kv):
        return S * chunk_decay + kv, S

    _, S_prev = lax.scan(step, jnp.zeros((B, H, dk, dv), jnp.float32), jnp.moveaxis(chunk_kv, 1, 0))
    S_prev = jnp.moveaxis(S_prev, 0, 1)
    q_w = jnp.exp((idx + 1.0)[:, None] * log_gamma[None, :])
    inter = jnp.einsum('bnihd,ih,bnhde->bnihe', qc, q_w, S_prev)
    return (intra + inter).reshape(B, L, H, dv)


def retention_mixer(rq, rk, rv, rg, decay_logit, cos, sin):
    B, L, _ = rq.shape
    q = apply_rotary(rq.reshape(B, L, RET_HEADS, RET_DK), cos, sin).astype(jnp.float32)
    k = apply_rotary(rk.reshape(B, L, RET_HEADS, RET_DK), cos, sin).astype(jnp.float32) * (RET_DK ** -0.5)
    v = rv.reshape(B, L, RET_HEADS, RET_DV).astype(jnp.float32)
    lg = jax.nn.log_sigmoid(decay_logit.astype(jnp.float32))
    fwd = retention_dir(q, k, v, lg[0], strict=False)
    bwd = jnp.flip(retention_dir(jnp.flip(q, 1), jnp.flip(k, 1), jnp.flip(v, 1), lg[1], strict=True), 1)
    o = fwd + bwd
    mu = jnp.mean(o, axis=-1, keepdims=True)
    oc = o - mu
    o = oc * lax.rsqrt(jnp.mean(oc * oc, axis=-1, keepdims=True) + EPS)
    o = o.reshape(B, L, RET_WIDTH).astype(rg.dtype)
    return jax.nn.silu(rg) * o


def axial_rotary(x, rc, rs, cc, cs):
    half = x.shape[-1] // 2
    return jnp.concatenate([apply_rotary(x[..., :half], rc, rs), apply_rotary(x[..., half:], cc, cs)], axis=-1)


def block_attention(q, k, v):
    B, L, H, hd = q.shape
    G = H // ATT_KV_HEADS
    nb = L // ATT_BLOCK
    qb = q.reshape(B, nb, ATT_BLOCK, ATT_KV_HEADS, G, hd).transpose(1, 0, 2, 3, 4, 5)
    scale = hd ** -0.5

    def one(qblk):
        s = jnp.einsum('bqkgd,bskd->bkgqs', qblk, k).astype(jnp.float32) * scale
        p = jax.nn.softmax(s, axis=-1).astype(v.dtype)
        return jnp.einsum('bkgqs,bskd->bqkgd', p, v)

    out = lax.map(one, qb)
    return out.transpose(1, 0, 2, 3, 4, 5).reshape(B, L, H * hd)


def attention_mixer(aq, ak, av, qn, kn, rc, rs, cc, cs):
    B, L, _ = aq.shape
    q = rmsnorm(aq.reshape(B, L, ATT_HEADS, ATT_HD), qn)
    k = rmsnorm(ak.reshape(B, L, ATT_KV_HEADS, ATT_HD), kn)
    v = av.reshape(B, L, ATT_KV_HEADS, ATT_HD)
    q = axial_rotary(q, rc, rs, cc, cs)
    k = axial_rotary(k, rc, rs, cc, cs)
    return block_attention(q, k, v)


def cross_attention(h, m, wq, wkv, wo):
    B, L, _ = h.shape
    M = m.shape[1]
    q = (h @ wq).reshape(B, L, X_HEADS, X_HD)
    kk, vv = jnp.split(m @ wkv, 2, axis=-1)
    kk = kk.reshape(B, M, X_HEADS, X_HD)
    vv = vv.reshape(B, M, X_HEADS, X_HD)
    s = jnp.einsum('blhd,bmhd->bhlm', q, kk).astype(jnp.float32) * (X_HD ** -0.5)
    p = jax.nn.softmax(s, axis=-1).astype(vv.dtype)
    o = jnp.einsum('bhlm,bmhd->blhd', p, vv).reshape(B, L, D_MODEL)
    return o @ wo


def trunk(x, mem, g_mix_pre, g_mix_post, w_in, hy_conv, hy_fw1, hy_fb1, hy_fw2, hy_fb2, hy_fw3, hy_bias,
          ret_decay, att_qnorm, att_knorm, w_branch, w_out, g_x_pre, g_x_post, g_mem, w_xq, w_xkv, w_xo,
          g_ff_pre, g_ff_post, w_ff1, w_ff2):
    B, L, _ = x.shape
    rows = L // GRID_W
    pos = jnp.arange(L)
    row_ids = jnp.repeat(jnp.arange(rows), GRID_W)
    col_ids = jnp.tile(jnp.arange(GRID_W), rows)
    ret_cos, ret_sin = rotary_tables(pos, RET_DK, RET_THETA)
    rc, rs = rotary_tables(row_ids, ATT_HD // 2, ROPE_THETA)
    cc, cs = rotary_tables(col_ids, ATT_HD // 2, ROPE_THETA)
    split_pts = _split_points(IN_SPLITS)
    for l in range(DEPTH):
        h = rmsnorm(x, g_mix_pre[l])
        proj = h @ w_in[l]
        hy_in, rq, rk, rv, rg, aq, ak, av, gates = jnp.split(proj, split_pts, axis=-1)
        K = hyena_spectrum(L, hy_fw1[l], hy_fb1[l], hy_fw2[l], hy_fb2[l], hy_fw3[l])
        y_h = hyena_mixer(hy_in, hy_conv[l], K, hy_bias[l])
        y_r = retention_mixer(rq, rk, rv, rg, ret_decay[l], ret_cos, ret_sin)
        y_a = attention_mixer(aq, ak, av, att_qnorm[l], att_knorm[l], rc, rs, cc, cs)
        wb = w_branch[l]
        p_h = y_h @ wb[:HY_WIDTH]
        p_r = y_r @ wb[HY_WIDTH:HY_WIDTH + RET_WIDTH]
        p_a = y_a @ wb[HY_WIDTH + RET_WIDTH:]
        g = jax.nn.sigmoid(gates.reshape(B, L, N_BRANCH, D_MODEL))
        merged = g[:, :, 0] * p_h + g[:, :, 1] * p_r + g[:, :, 2] * p_a
        x = x + rmsnorm(merged @ w_out[l], g_mix_post[l])
        h = rmsnorm(x, g_x_pre[l])
        m = rmsnorm(mem, g_mem[l])
        x = x + rmsnorm(cross_attention(h, m, w_xq[l], w_xkv[l], w_xo[l]), g_x_post[l])
        h = rmsnorm(x, g_ff_pre[l])
        u = jnp.square(jax.nn.relu(h @ w_ff1[l]))
        x = x + rmsnorm(u @ w_ff2[l], g_ff_post[l])
    return x


def setup_inputs(seed: int = 0) -> dict:
    key = jax.random.key(seed)
    ks = iter(jax.random.split(key, 40))

    def nrm(shape, scale):
        return jax.random.normal(next(ks), shape, jnp.float32) * scale

    def gain(shape):
        return 1.0 + nrm(shape, 0.01)

    base = 1.0 - 2.0 ** (-5.0 - jnp.arange(RET_HEADS, dtype=jnp.float32))
    logit = jnp.log(base / (1.0 - base))
    d = {}
    d['x_prompt'] = nrm((BATCH, SEQ, D_MODEL), 1.0)
    d['x_sample'] = nrm((DEC_BATCH, DEC_SEQ, D_MODEL), 1.0)
    d['mem_prompt'] = nrm((BATCH, N_MEM, D_MODEL), 1.0)
    d['mem_sample'] = nrm((DEC_BATCH, N_MEM, D_MODEL), 1.0)
    d['g_mix_pre'] = gain((DEPTH, D_MODEL))
    d['g_mix_post'] = gain((DEPTH, D_MODEL))
    d['w_in'] = nrm((DEPTH, D_MODEL, IN_COLS), D_MODEL ** -0.5)
    d['hy_conv'] = nrm((DEPTH, HY_SHORT, 3 * HY_WIDTH), HY_SHORT ** -0.5)
    d['hy_fw1'] = nrm((DEPTH, HY_EMB, HY_FH), HY_EMB ** -0.5)
    d['hy_fb1'] = nrm((DEPTH, HY_FH), 0.02)
    d['hy_fw2'] = nrm((DEPTH, HY_FH, HY_FH), HY_FH ** -0.5)
    d['hy_fb2'] = nrm((DEPTH, HY_FH), 0.02)
    d['hy_fw3'] = nrm((DEPTH, HY_FH, 2 * HY_WIDTH), 0.02)
    d['hy_bias'] = nrm((DEPTH, HY_WIDTH), 1.0)
    d['ret_decay'] = logit[None, None, :] + nrm((DEPTH, 2, RET_HEADS), 0.1)
    d['att_qnorm'] = gain((DEPTH, ATT_HD))
    d['att_knorm'] = gain((DEPTH, ATT_HD))
    d['w_branch'] = nrm((DEPTH, MIX_WIDTH, D_MODEL), HY_WIDTH ** -0.5)
    d['w_out'] = nrm((DEPTH, D_MODEL, D_MODEL), D_MODEL ** -0.5)
    d['g_x_pre'] = gain((DEPTH, D_MODEL))
    d['g_x_post'] = gain((DEPTH, D_MODEL))
    d['g_mem'] = gain((DEPTH, D_MODEL))
    d['w_xq'] = nrm((DEPTH, D_MODEL, D_MODEL), D_MODEL ** -0.5)
    d['w_xkv'] = nrm((DEPTH, D_MODEL, 2 * D_MODEL), D_MODEL ** -0.5)
    d['w_xo'] = nrm((DEPTH, D_MODEL, D_MODEL), D_MODEL ** -0.5)
    d['g_ff_pre'] = gain((DEPTH, D_MODEL))
    d['g_ff_post'] = gain((DEPTH, D_MODEL))
    d['w_ff1'] = nrm((DEPTH, D_MODEL, D_FF), D_MODEL ** -0.5)
    d['w_ff2'] = nrm((DEPTH, D_FF, D_MODEL), D_FF ** -0.5)
    return d


def reference(x_prompt, x_sample, mem_prompt, mem_sample, g_mix_pre, g_mix_post, w_in, hy_conv, hy_fw1, hy_fb1,
              hy_fw2, hy_fb2, hy_fw3, hy_bias, ret_decay, att_qnorm, att_knorm, w_branch, w_out, g_x_pre, g_x_post,
              g_mem, w_xq, w_xkv, w_xo, g_ff_pre, g_ff_post, w_ff1, w_ff2):
    y_prompt = trunk(x_prompt, mem_prompt, g_mix_pre, g_mix_post, w_in, hy_conv, hy_fw1, hy_fb1, hy_fw2, hy_fb2,
                     hy_fw3, hy_bias, ret_decay, att_qnorm, att_knorm, w_branch, w_out, g_x_pre, g_x_post, g_mem,
                     w_xq, w_xkv, w_xo, g_ff_pre, g_ff_post, w_ff1, w_ff2)
    y_sample = trunk(x_sample, mem_sample, g_mix_pre, g_mix_post, w_in, hy_conv, hy_fw1, hy_fb1, hy_fw2, hy_fb2,
                     hy_fw3, hy_bias, ret_decay, att_qnorm, att_knorm, w_branch, w_out, g_x_pre, g_x_post, g_mem,
                     w_xq, w_xkv, w_xo, g_ff_pre, g_ff_post, w_ff1, w_ff2)
    return (y_prompt, y_sample)
```

```python
import contextlib
import math
import numpy as np
import concourse.bass as bass
import concourse.mybir as mybir
from concourse.bass_utils import run_bass_kernel_spmd

F32 = mybir.dt.float32
BF16 = mybir.dt.bfloat16
AF = mybir.ActivationFunctionType
ALU = mybir.AluOpType
AX = mybir.AxisListType


class Tok:
    __slots__ = ("w", "r", "name", "owner")

    def __init__(self, name="", owner=None):
        self.w = None
        self.r = []
        self.name = name
        self.owner = owner


class Buf:
    def __init__(self, handle, name, is_dram=False):
        self.h = handle
        self.name = name
        self.t = Tok(name, self)
        self.dsem = {}
        self.subs = {}
        self.is_dram = is_dram
        self._ap = handle.ap() if is_dram else None

    def __getitem__(self, key):
        if self.is_dram:
            return self._ap[key]
        return self.h[key]

    def tok(self, i):
        if i not in self.subs:
            self.subs[i] = Tok(f"{self.name}.{i}", self)
        return self.subs[i]


class Prog:
    ENG = ("pe", "act", "dve", "pool", "sp")

    def __init__(self, nc):
        self.nc = nc
        self.stack = contextlib.ExitStack()
        self.ops = {e: [] for e in self.ENG}
        self.NDS = 40
        self.streams = list(self.ENG) + [f"ds{i}" for i in range(self.NDS)]
        self.free_ds = {"sp": [f"ds{i}" for i in range(0, 26)], "pool": [f"ds{i}" for i in range(26, self.NDS)]}
        self.scope_bufs = [[]]
        self.sem = {}
        for s in self.streams:
            self.sem[s] = self.stack.enter_context(nc.semaphore("sem_" + s))
        self.cnt = {s: 0 for s in self.streams}
        self.seen = {e: {s: 0 for s in self.streams} for e in self.ENG}
        self.nops = 0
        self.nwaits = 0

    def dram_in(self, name, shape, dt):
        return Buf(self.nc.dram_tensor(name, list(shape), dt, kind="ExternalInput"), name, True)

    def dram_out(self, name, shape, dt):
        return Buf(self.nc.dram_tensor(name, list(shape), dt, kind="ExternalOutput"), name, True)

    def dram_tmp(self, name, shape, dt):
        return Buf(self.nc.dram_tensor(name, list(shape), dt), name, True)

    def sb(self, name, shape, dt):
        self.uid = getattr(self, "uid", 0) + 1
        name = f"s{self.uid}_{name}"
        b = self._sb(name, shape, dt)
        self.scope_bufs[-1].append(b)
        return b

    def _sb(self, name, shape, dt):
        return Buf(self.stack.enter_context(self.nc.sbuf_tensor(name, list(shape), dt)), name)

    def ps(self, name, shape, dt):
        name = "p_" + name
        return Buf(self.stack.enter_context(self.nc.psum_tensor(name, list(shape), dt)), name)

    def op(self, eng, fn, reads=(), writes=(), sig=True, dma=False):
        if getattr(self, 'maxops', None) and self.nops >= self.maxops and sig:
            raise _Stop()
        if dma:
            owner = None
            for t in list(writes) + list(reads):
                if t.owner is not None and not t.owner.is_dram:
                    owner = t.owner
                    break
            assert owner is not None, "DMA needs an SBUF-side token"
            qk = "pool" if eng == "pool" else "sp"
            if qk not in owner.dsem:
                owner.dsem[qk] = self.free_ds[qk].pop(0)
            stream = owner.dsem[qk]
        else:
            stream = eng
        deps = {}
        for t in reads:
            if t.w is not None:
                s, c = t.w
                deps[s] = max(deps.get(s, 0), c)
        for t in writes:
            if t.w is not None:
                s, c = t.w
                deps[s] = max(deps.get(s, 0), c)
            for (s, c) in t.r:
                deps[s] = max(deps.get(s, 0), c)
        for s, c in deps.items():
            if s == "pe" and eng == "pe":
                continue
            if self.seen[eng][s] >= c:
                continue
            self.seen[eng][s] = c
            sem = self.sem[s]
            self.ops[eng].append(lambda e, sem=sem, c=c: e.wait_ge(sem, c))
            self.nwaits += 1
        if sig:
            self.cnt[stream] += 1 if not dma else 16
            myc = self.cnt[stream]
            sem = self.sem[stream]
            inc = 16 if dma else 1
            self.ops[eng].append(lambda e, fn=fn, sem=sem, inc=inc: fn(e).then_inc(sem, inc))
        else:
            myc = self.cnt[stream] + (16 if dma else 1)
            self.ops[eng].append(lambda e, fn=fn: fn(e))
        for t in reads:
            t.r.append((stream, myc))
        for t in writes:
            t.w = (stream, myc)
            t.r = []
        self.nops += 1

    def dma(self, eng, out_ap, in_ap, reads, writes, **kw):
        self.op(eng, lambda e: e.dma_start(out=out_ap, in_=in_ap, **kw), reads, writes, dma=True)

    def mm(self, out_ap, lhsT, rhs, start, stop, reads, writes, sig=None):
        if sig is None:
            sig = stop
        self.op("pe", lambda e: e.matmul(out_ap, lhsT, rhs, start=start, stop=stop), reads, writes, sig=sig)

    def finish(self):
        nc = self.nc
        for s in self.streams:
            if self.cnt[s] > 0 and s != "sp":
                c = self.cnt[s]
                sem = self.sem[s]
                self.ops["sp"].append(lambda e, sem=sem, c=c: e.wait_ge(sem, c))
        ops = self.ops
        with nc.Block() as block:
            @block.tensor
            def _(e):
                for f in ops["pe"]:
                    f(e)

            @block.scalar
            def _(e):
                for f in ops["act"]:
                    f(e)

            @block.vector
            def _(e):
                for f in ops["dve"]:
                    f(e)

            @block.gpsimd
            def _(e):
                for f in ops["pool"]:
                    f(e)

            @block.sync
            def _(e):
                for f in ops["sp"]:
                    f(e)
        self.stack.close()


L_DEPTH = 4
DBG = {}
NT = 4096
D = 1024
PI = math.pi
EPS = 1e-6
SC128 = 128.0 ** -0.5


class _Stop(Exception):
    pass


def build_program(depth=L_DEPTH, dbg=False, stop_after=None):
    nc = bass.Bass("TRN2", target_bir_lowering=False)
    P = Prog(nc)
    P.maxops = DBG.get('maxops')
    op = P.op

    def checkpoint(name):
        if stop_after == name:
            raise _Stop()

    def act(out, in_, func, R, W, **kw):
        op("act", lambda e: e.activation(out, in_, func, **kw), R, W)

    def tt(out, a, b, o, R, W):
        op("dve", lambda e: e.tensor_tensor(out, a, b, o), R, W)

    def ts(out, a, s1, s2, o0, o1, R, W):
        op("dve", lambda e: e.tensor_scalar(out, a, s1, s2, o0, o1), R, W)

    def stt(out, a, s, b, o0, o1, R, W):
        op("dve", lambda e: e.scalar_tensor_tensor(out, a, s, b, o0, o1), R, W)

    def mset(out, val, W):
        op("dve", lambda e: e.memset(out, val), [], W)

    def recip(out, in_, R, W):
        op("dve", lambda e: e.reciprocal(out, in_), R, W)

    def cpv(out, in_, R, W):
        op("dve", lambda e: e.tensor_copy(out, in_), R, W)

    def cpa(out, in_, R, W):
        op("act", lambda e: e.activation(out, in_, AF.Identity), R, W)

    evs = [0]

    def evac(out, in_, R, W):
        evs[0] ^= 1
        (cpa if evs[0] else cpv)(out, in_, R, W)

    def ld(out, in_, W, eng="sp"):
        P.dma(eng, out, in_, [], W)

    def st(out, in_, R, eng="pool"):
        P.dma(eng, out, in_, R, [])

    def barrier():
        for e in P.ENG:
            for s in P.streams:
                c = P.cnt[s]
                if c > P.seen[e][s]:
                    P.seen[e][s] = c
                    sem = P.sem[s]
                    P.ops[e].append(lambda en, sem=sem, c=c: en.wait_ge(sem, c))

    class scope:
        def __enter__(self):
            self.saved = P.stack
            P.stack = contextlib.ExitStack()
            P.scope_bufs.append([])
            return self

        def __exit__(self, *a):
            barrier()
            for b in P.scope_bufs.pop():
                for qk, ds in b.dsem.items():
                    P.free_ds[qk].append(ds)
                b.dsem = {}
            P.stack.close()
            P.stack = self.saved
            return False

    di = P.dram_in
    x_in = di("x", [NT, D], F32)
    mem_in = di("mem", [2, 256, D], F32)
    w_in = di("w_in", [depth, D, 7680], F32)
    w_br = di("w_branch", [depth, 1536, D], F32)
    w_out = di("w_out", [depth, D, D], F32)
    w_xq = di("w_xq", [depth, D, D], F32)
    w_xkv = di("w_xkv", [depth, D, 2 * D], F32)
    w_xo = di("w_xo", [depth, D, D], F32)
    w_ff1 = di("w_ff1", [depth, D, 4 * D], F32)
    w_ff2 = di("w_ff2", [depth, 4 * D, D], F32)
    convw = di("convw", [128, depth * 36], F32)
    fw1 = di("hy_fw1", [depth, 33, 64], F32)
    fb1 = di("fb1c", [64, depth], F32)
    fw2 = di("hy_fw2", [depth, 64, 64], F32)
    fb2 = di("fb2c", [64, depth], F32)
    fw3 = di("hy_fw3", [depth, 64, 1024], F32)
    hbias = di("hbiasc", [128, depth * 4], F32)
    rdec = di("rdec", [depth, 8], F32)
    qn_in = di("att_qnorm", [depth, 128], F32)
    kn_in = di("att_knorm", [depth, 128], F32)
    gpre = di("gpre", [128, depth * 32], F32)
    gpost = di("gpost", [depth * 3, D], F32)
    ident_in = di("ident", [128, 128], F32)
    ctab_in = di("ctab", [128, 6 * 128 + 8], F32)
    cosR = di("cosR", [NT, 64], F32)
    sinR = di("sinR", [NT, 64], F32)
    axc = di("axc", [NT, 64], F32)
    axs = di("axs", [NT, 64], F32)
    zemb = di("zemb", [33, NT], F32)
    tcol_in = di("tcol", [128, 32], F32)
    maskb_in = di("maskb", [128, 32], F32)
    delta_in = di("delta", [1, 512], F32)
    misc_in = di("misc", [128, 8], F32)
    djq_in = di("djq", [128, 256], F32)
    mjq_all = di("mjq", [128, 512], F32)
    base_in = di("base", [128, 512], F32)
    Cf = di("Cf", [32, 128, 32 * 128], BF16)
    Sf = di("Sf", [32, 128, 32 * 128], BF16)
    Ci = di("Ci", [8, 4, 128, 8 * 512], BF16)
    Si = di("Si", [8, 4, 128, 8 * 512], BF16)
    y_out = P.dram_out("y", [NT, D], F32)

    dt_ = P.dram_out if dbg else P.dram_tmp
    xres = dt_("xres", [NT, D], F32)
    hTd = dt_("hTd", [8, 128, NT], BF16)
    hyTd = dt_("hyTd", [1536, NT], BF16)
    rqd = dt_("rqd", [NT, 512], BF16)
    rkd = dt_("rkd", [NT, 512], BF16)
    rvd = dt_("rvd", [NT, 512], BF16)
    rgT = dt_("rgT", [512, NT], BF16)
    x0cd = dt_("x0cd", [512, NT], BF16)
    zTd = dt_("zTd", [512, NT], BF16)
    aqd = dt_("aqd", [NT, 512], BF16)
    akd = dt_("akd", [NT, 256], BF16)
    avd = dt_("avd", [NT, 256], BF16)
    yT3 = dt_("yT3", [1536, NT], BF16)
    ident = P.sb("ident", [128, 128], BF16)
    ones = P.sb("ones", [128, 128], BF16)
    ctab = P.sb("ctab", [128, 6 * 128 + 8], F32)
    misc = P.sb("misc", [128, 8], F32)
    rot = [P.ps(f"rot{i}", [128, 512], F32) for i in range(3)]
    accb = [P.ps(f"acc{i}", [128, 512], F32) for i in range(3)]
    trbs = [P.ps(f"trb{i}", [128, 512], F32) for i in range(2)]
    quad = [rot[0], rot[1], rot[2], accb[2]]
    roti = [0]

    rot_pool = [list(rot)]

    def rotp():
        roti[0] = (roti[0] + 1) % len(rot_pool[0])
        return rot_pool[0][roti[0]]

    tri = [DBG.get('tri0', 0)]

    def tr_half():
        tri[0] ^= 1
        h = tri[0]
        return trbs[h][:, :], trbs[h].t

    ld(ident[:, :], ident_in[:, :], [ident.t], eng="pool")
    ld(ctab[:, :], ctab_in[:, :], [ctab.t])
    ld(misc[:, :], misc_in[:, :], [misc.t])
    mset(ones[:, :], 1.0, [ones.t])
    R1, R2, M1, M2, NP1, NM = [ctab[:, i * 128:(i + 1) * 128] for i in range(6)]
    pcol = ctab[:, 768:769]
    pcol_r = ctab[:, 769:770]
    c128 = ctab[:, 770:771]
    mcoef = misc[:, 0:1]

    def transposes(srcs, R, dst, W, scale_col=None):
        n = len(srcs)
        ph, ptok = tr_half()
        for i, s_ in enumerate(srcs):
            op("pe", lambda e, s_=s_, i=i: e.matmul(ph[:, i * 128:(i + 1) * 128], s_, ident[:, :], start=True, stop=True),
               list(R) + [ident.t], [ptok], sig=(i == n - 1))
        if scale_col is None:
            evac(dst, ph[:, 0:n * 128], [ptok], W)
        else:
            ts(dst, ph[:, 0:n * 128], scale_col, 0.0, ALU.mult, ALU.add, [ptok], W)

    def rstd_from_ss(ss, n, inv_n, R_W):
        ts(ss[:, 0:n], ss[:, 0:n], inv_n, EPS, ALU.mult, ALU.add, R_W, R_W)
        act(ss[:, 0:n], ss[:, 0:n], AF.Ln, R_W, R_W)
        act(ss[:, 0:n], ss[:, 0:n], AF.Exp, R_W, R_W, scale=-0.5)

    gp_tok = [None]

    def norm_T(xt_ap, xtoks, gcol, hT, col0, junk, hb, ss):
        mset(ss[:, 0:1], 0.0, [ss.t])
        act(junk[:, :], xt_ap, AF.Square, xtoks, [junk.t, ss.t], accum_out=ss[:, 0:1])
        rstd_from_ss(ss, 1, 1.0 / D, [ss.t])
        ts(hb[:, :], xt_ap, ss[:, 0:1], 0.0, ALU.mult, ALU.add, xtoks + [ss.t], [hb.t])
        for half in range(2):
            ph, ptok = tr_half()
            for i in range(4):
                kc = half * 4 + i
                op("pe", lambda e, kc=kc, i=i, ph=ph: e.matmul(ph[:, i * 128:(i + 1) * 128], hb[:, kc * 128:(kc + 1) * 128], ident[:, :], start=True, stop=True),
                   [hb.t, ident.t], [ptok], sig=(i == 3))
            for i in range(4):
                kc = half * 4 + i
                ts(hT[:, kc, col0:col0 + 128], ph[:, i * 128:(i + 1) * 128], gcol[:, kc:kc + 1], 0.0, ALU.mult, ALU.add,
                   [ptok, gp_tok[0]], [hT.t])

    def bcast_mid(buf, off, F, inner, H):
        return bass.AP(buf.h, off, [[F, 128], [0, H]] + inner)

    def rotary(src, H, A, Wd, ctb, stb, toff, TF, dst_ap, dtok, t1, t2):
        xv = src[:, 0:H * 128].rearrange("p (h a two w) -> p h a two w", h=H, a=A, two=2, w=Wd)
        dv = dst_ap.rearrange("p (h a two w) -> p h a two w", h=H, a=A, two=2, w=Wd)
        x1, x2 = xv[:, :, :, 0, :], xv[:, :, :, 1, :]
        c = bcast_mid(ctb, toff, TF, [[Wd, A], [1, Wd]], H)
        s = bcast_mid(stb, toff, TF, [[Wd, A], [1, Wd]], H)
        a1 = t1[:, 0:H * 64].rearrange("p (h a w) -> p h a w", h=H, a=A, w=Wd)
        a2 = t2[:, 0:H * 64].rearrange("p (h a w) -> p h a w", h=H, a=A, w=Wd)
        tb = [ctb.t, stb.t]
        tt(a1, x1, c, ALU.mult, [src.t] + tb, [t1.t])
        tt(a2, x2, s, ALU.mult, [src.t] + tb, [t2.t])
        tt(dv[:, :, :, 0, :], a1, a2, ALU.subtract, [t1.t, t2.t], [dtok])
        tt(a1, x1, s, ALU.mult, [src.t] + tb, [t1.t])
        tt(a2, x2, c, ALU.mult, [src.t] + tb, [t2.t])
        tt(dv[:, :, :, 1, :], a1, a2, ALU.add, [t1.t, t2.t], [dtok])

    def wload(ring, ridx, wsrc2d, r0, nkc, c0, ncols):
        ridx[0] = (ridx[0] + 1) % len(ring)
        wb = ring[ridx[0]]
        src = wsrc2d[r0:r0 + nkc * 128, c0:c0 + ncols].rearrange("(kc p) c -> p kc c", p=128)
        P.dma("sp", wb[:, 0:nkc, 0:ncols], src, [], [wb.t])
        return wb


    wspec = [("w_in", w_in, D, 7680), ("w_br", w_br, 1536, D), ("w_out", w_out, D, D), ("w_xq", w_xq, D, D),
             ("w_xkv", w_xkv, D, 2 * D), ("w_xo", w_xo, D, D), ("w_ff1", w_ff1, D, 4 * D), ("w_ff2", w_ff2, 4 * D, D)]
    wbf = {}
    for nm, src_, K_, N_ in wspec:
        wbf[nm] = P.dram_tmp("bf_" + nm, [depth, K_, N_], BF16)
    with scope():
        stg32 = [P.sb(f"stg32_{i}", [128, 8192], F32) for i in range(2)]
        stg16 = [P.sb(f"stg16_{i}", [128, 8192], BF16) for i in range(2)]
        ci = [0]
        for nm, src_, K_, N_ in wspec:
            nkc_all = K_ // 128
            per = max(1, 8192 // N_)
            for l in range(depth):
                for k0 in range(0, nkc_all, per):
                    nk = min(per, nkc_all - k0)
                    ci[0] ^= 1
                    a32, a16 = stg32[ci[0]], stg16[ci[0]]
                    v32 = a32[:, 0:nk * N_].rearrange("p (k n) -> p k n", k=nk)
                    v16 = a16[:, 0:nk * N_].rearrange("p (k n) -> p k n", k=nk)
                    ld(v32, src_[l][k0 * 128:(k0 + nk) * 128, :].rearrange("(k p) n -> p k n", p=128), [a32.t])
                    if ci[0]:
                        cpa(a16[:, 0:nk * N_], a32[:, 0:nk * N_], [a32.t], [a16.t])
                    else:
                        cpv(a16[:, 0:nk * N_], a32[:, 0:nk * N_], [a32.t], [a16.t])
                    st(wbf[nm][l][k0 * 128:(k0 + nk) * 128, :].rearrange("(k p) n -> p k n", p=128), v16, [a16.t])

    try:
      for l in range(depth):
        xsrc = x_in if l == 0 else xres
        xdst = y_out if l == depth - 1 else xres
        with scope():
            gp = P.sb("gp", [128, 32], F32)
            ld(gp[:, :], gpre[:, l * 32:(l + 1) * 32], [gp.t])
            gp_tok[0] = gp.t
            lgt = P.sb("lgt", [128, 8], F32)
            ld(lgt[:, :], rdec[l:l + 1, :].partition_broadcast(128), [lgt.t])
            act(lgt[:, :], lgt[:, :], AF.Exp, [lgt.t], [lgt.t], scale=-1.0)
            act(lgt[:, :], lgt[:, :], AF.Ln, [lgt.t], [lgt.t], bias=1.0)
            ts(lgt[:, :], lgt[:, :], -1.0, 0.0, ALU.mult, ALU.add, [lgt.t], [lgt.t])
            kkT = P.sb("kkT", [128, 8, 512], BF16)
            vv = P.sb("vv", [128, 4, 1024], BF16)

            checkpoint('S0')
            rot_pool[0] = rot + accb
            with scope():
                cR = P.sb("cR", [128, 32 * 64], F32)
                sR = P.sb("sR", [128, 32 * 64], F32)
                aC = P.sb("aC", [128, 32 * 64], F32)
                aS = P.sb("aS", [128, 32 * 64], F32)
                for tb_, src_ in ((cR, cosR), (sR, sinR), (aC, axc), (aS, axs)):
                    ld(tb_[:, :].rearrange("p (g w) -> p g w", g=32), src_[:, :].rearrange("(g p) w -> p g w", p=128), [tb_.t])
                qg = P.sb("qg", [128, 128], F32)
                kg = P.sb("kg", [128, 128], F32)
                ld(qg[:, :], qn_in[l:l + 1, :].partition_broadcast(128), [qg.t])
                ld(kg[:, :], kn_in[l:l + 1, :].partition_broadcast(128), [kg.t])
                checkpoint('A0')
                xt = [P.sb(f"xtA{i}", [128, D], F32) for i in range(2)]
                junk = P.sb("junkA", [128, D], F32)
                hb = P.sb("hbA", [128, D], BF16)
                ss = P.sb("ssA", [128, 8], F32)
                hT = [P.sb(f"hTA{i}", [128, 8, 512], BF16) for i in range(2)]
                ring = [P.sb(f"wrA{i}", [128, 8, 512], BF16) for i in range(3)]
                ridx = [0]
                obf = [P.sb(f"obA{i}", [128, 512], BF16) for i in range(3)]
                o32 = [P.sb(f"o32A{i}", [128, 512], F32) for i in range(2)]
                xn32 = P.sb("xn32", [128, 512], F32)
                t1 = P.sb("t1A", [128, 256], F32)
                t2 = P.sb("t2A", [128, 256], F32)
                oi = [0]
                for T in range(DBG.get('T', 8)):
                    h_ = hT[T % 2]
                    for s in range(4):
                        xb_ = xt[s % 2]
                        ld(xb_[:, :], xsrc[T * 512 + s * 128:T * 512 + (s + 1) * 128, :], [xb_.t])
                        norm_T(xb_[:, :], [xb_.t], gp[:, 0:8], h_, s * 128, junk, hb, ss)
                    checkpoint('A1')
                    st(hTd[:, :, T * 512:(T + 1) * 512].rearrange("kc p t -> p kc t"), h_[:, :, :], [h_.t])
                    checkpoint('A2')
                    for grp in DBG.get('G', range(9)):
                        wb = wload(ring, ridx, wbf["w_in"][l], 0, 8, grp * 512, 512)
                        if grp < 3 or grp == 6:
                            for f in range(4):
                                ps = rotp()
                                for kc in range(8):
                                    P.mm(ps[:, :], wb[:, kc, f * 128:(f + 1) * 128], h_[:, kc, :], kc == 0, kc == 7, [wb.t, h_.t], [ps.t])
                                oi[0] = (oi[0] + 1) % 3
                                ob = obf[oi[0]]
                                evac(ob[:, :], ps[:, :], [ps.t], [ob.t])
                                r0 = grp * 512 + f * 128
                                if grp == 6:
                                    st(rgT[f * 128:(f + 1) * 128, T * 512:(T + 1) * 512], ob[:, :], [ob.t])
                                else:
                                    st(hyTd[r0:r0 + 128, T * 512:(T + 1) * 512], ob[:, :], [ob.t])
                        else:
                            for s in range(4):
                                g_ = T * 4 + s
                                tsl = slice(g_ * 128, (g_ + 1) * 128)
                                ps = rotp()
                                for kc in range(8):
                                    P.mm(ps[:, :], h_[:, kc, s * 128:(s + 1) * 128], wb[:, kc, :], kc == 0, kc == 7, [wb.t, h_.t], [ps.t])
                                oi[0] = (oi[0] + 1) % 3
                                ob = obf[oi[0]]
                                if grp == 5:
                                    evac(ob[:, :], ps[:, :], [ps.t], [ob.t])
                                    st(rvd[tsl, :], ob[:, :], [ob.t])
                                    continue
                                o3 = o32[s % 2]
                                cpa(o3[:, :], ps[:, :], [ps.t], [o3.t])
                                if grp in (3, 4):
                                    rotary(o3, 4, 1, 64, cR, sR, g_ * 64, 32 * 64, ob[:, :], ob.t, t1, t2)
                                    st((rqd if grp == 3 else rkd)[tsl, :], ob[:, :], [ob.t])
                                else:
                                    nh = 4 if grp == 7 else 2
                                    gt = qg if grp == 7 else kg
                                    mset(ss[:, 0:4], 0.0, [ss.t])
                                    for h in range(nh):
                                        act(junk[:, 0:128], o3[:, h * 128:(h + 1) * 128], AF.Square, [o3.t], [junk.t, ss.t], accum_out=ss[:, h:h + 1])
                                    rstd_from_ss(ss, nh, 1.0 / 128, [ss.t])
                                    for h in range(nh):
                                        stt(xn32[:, h * 128:(h + 1) * 128], o3[:, h * 128:(h + 1) * 128], ss[:, h:h + 1], gt[:, :], ALU.mult, ALU.mult,
                                            [o3.t, ss.t, gt.t], [xn32.t])
                                    rotary(xn32, nh, 2, 32, aC, aS, g_ * 64, 32 * 64, ob[:, 0:nh * 128], ob.t, t1, t2)
                                    if grp == 7:
                                        st(aqd[tsl, :], ob[:, :], [ob.t])
                                    else:
                                        cpv(ob[:, 256:512], o3[:, 256:512], [o3.t], [ob.t])
                                        st(akd[tsl, :], ob[:, 0:256], [ob.t])
                                        st(avd[tsl, :], ob[:, 256:512], [ob.t])
            rot_pool[0] = list(rot)
            checkpoint('A')
            with scope():
                w1s = P.sb("w1s", [33, 64], F32)
                w2s = P.sb("w2s", [64, 64], F32)
                w3s = P.sb("w3s", [64, 1024], F32)
                b1s = P.sb("b1s", [64, depth], F32)
                b2s = P.sb("b2s", [64, depth], F32)
                h2T = P.sb("h2T", [64, NT], F32)
                cw = P.sb("cw", [128, 36], F32)
                cwf = P.sb("cwf", [128, 36], F32)
                hbs = P.sb("hbs", [128, 4], F32)
                tcol = P.sb("tcol", [128, 32], F32)
                mkb = P.sb("mkb", [128, 32], F32)
                dlt = P.sb("dlt", [128, 512], F32)
                ld(w1s[:, :], fw1[l], [w1s.t]); ld(w2s[:, :], fw2[l], [w2s.t])
                ld(w3s[:, :], fw3[l], [w3s.t]); ld(b1s[:, :], fb1[:, :], [b1s.t]); ld(b2s[:, :], fb2[:, :], [b2s.t])
                ld(cw[:, :], convw[:, l * 36:(l + 1) * 36], [cw.t]); ld(hbs[:, :], hbias[:, l * 4:(l + 1) * 4], [hbs.t])
                ld(tcol[:, :], tcol_in[:, :], [tcol.t]); ld(mkb[:, :], maskb_in[:, :], [mkb.t])
                ld(dlt[:, :], delta_in[0:1, :].partition_broadcast(128), [dlt.t])
                ts(tcol[:, :], tcol[:, :], -1.0, 0.0, ALU.mult, ALU.add, [tcol.t], [tcol.t])
                stt(cwf[:, :], cw[:, :], mcoef, cw[:, :], ALU.mult, ALU.subtract, [cw.t, misc.t], [cwf.t])

                swt_ref = [None]

                def sin_layer(dst, wmat, kdim, src, bcol, btok):
                    for jb in range(8):
                        ps = rotp()
                        P.mm(ps[0:64, :], wmat[0:kdim, :], src[0:kdim, jb * 512:(jb + 1) * 512], True, True, [wmat.t, src.t], [ps.t])
                        a_, m1_, m2_ = swt_ref[0]
                        ts(a_[:, :], ps[0:64, :], bcol, 0.0, ALU.add, ALU.add, [ps.t, btok], [a_.t])
                        ts(m1_[:, :], a_[:, :], PI, -2 * PI, ALU.is_gt, ALU.mult, [a_.t], [m1_.t])
                        ts(m2_[:, :], a_[:, :], -PI, 2 * PI, ALU.is_lt, ALU.mult, [a_.t], [m2_.t])
                        tt(a_[:, :], a_[:, :], m1_[:, :], ALU.add, [a_.t, m1_.t], [a_.t])
                        tt(a_[:, :], a_[:, :], m2_[:, :], ALU.add, [a_.t, m2_.t], [a_.t])
                        ts(a_[:, :], a_[:, :], -3.14159, 3.14159, ALU.max, ALU.min, [a_.t], [a_.t])
                        act(dst[:, jb * 512:(jb + 1) * 512], a_[:, :], AF.Sin, [a_.t], [dst.t])
                with scope():
                    zE = P.sb("zE", [33, NT], F32)
                    h1T = P.sb("h1T", [64, NT], F32)
                    swt = [P.sb(f"swt{i}", [64, 512], F32) for i in range(3)]
                    swt_ref[0] = swt
                    ld(zE[:, :], zemb[:, :], [zE.t])
                    sin_layer(h1T, w1s, 33, zE, b1s[:, l:l + 1], b1s.t)
                    sin_layer(h2T, w2s, 64, h1T, b2s[:, l:l + 1], b2s.t)

                checkpoint('H0')
                for cg in range(2):
                    with scope():
                        zkc = P.sb("zkc", [128, 32, 512], BF16)
                        zks = P.sb("zks", [128, 32, 512], BF16)
                        Pr = P.sb("Pr", [128, 32, 256], BF16)
                        Pi = P.sb("Pi", [128, 32, 256], BF16)
                        with scope():
                            raw = [P.sb(f"raw{i}", [128, NT], BF16) for i in range(3)]
                            tA = P.sb("tA", [128, NT], F32)
                            tB = P.sb("tB", [128, NT], F32)
                            zb = P.sb("zb", [128, NT], BF16)
                            for cc in range(2):
                                c4 = cg * 2 + cc
                                for k3 in range(3):
                                    r0 = k3 * 512 + c4 * 128
                                    ld(raw[k3][:, :], hyTd[r0:r0 + 128, :], [raw[k3].t])

                                def conv(dst, src, k3):
                                    wc = lambda j: cw[:, j * 12 + k3 * 4 + c4:j * 12 + k3 * 4 + c4 + 1]
                                    wf = lambda j: cwf[:, j * 12 + k3 * 4 + c4:j * 12 + k3 * 4 + c4 + 1]
                                    R_ = [src.t, cw.t, cwf.t]
                                    ts(dst[:, :], src[:, :], wc(1), 0.0, ALU.mult, ALU.add, R_, [dst.t])
                                    stt(dst[:, 1:NT], src[:, 0:NT - 1], wc(0), dst[:, 1:NT], ALU.mult, ALU.add, R_ + [dst.t], [dst.t])
                                    stt(dst[:, 0:NT - 1], src[:, 1:NT], wc(2), dst[:, 0:NT - 1], ALU.mult, ALU.add, R_ + [dst.t], [dst.t])
                                    stt(dst[:, 2048:2049], src[:, 2047:2048], wf(0), dst[:, 2048:2049], ALU.mult, ALU.add, R_ + [dst.t], [dst.t])
                                    stt(dst[:, 2047:2048], src[:, 2048:2049], wf(2), dst[:, 2047:2048], ALU.mult, ALU.add, R_ + [dst.t], [dst.t])
                                conv(tA, raw[0], 0)
                                cpa(zb[:, :], tA[:, :], [tA.t], [zb.t])
                                st(x0cd[c4 * 128:(c4 + 1) * 128, :], zb[:, :], [zb.t])
                                conv(tA, raw[1], 1)
                                conv(tB, raw[2], 2)
                                tt(zb[:, :], tA[:, :], tB[:, :], ALU.mult, [tA.t, tB.t], [zb.t])
                                st(zTd[c4 * 128:(c4 + 1) * 128, :], zb[:, :], [zb.t])
                                for t4 in range(8):
                                    srcs = [zb[:, (t4 * 4 + i) * 128:(t4 * 4 + i + 1) * 128] for i in range(4)]
                                    dst = zkc[:, t4 * 4:(t4 + 1) * 4, cc * 128:(cc + 1) * 128]
                                    dst2 = zks[:, t4 * 4:(t4 + 1) * 4, cc * 128:(cc + 1) * 128]
                                    ph, ptok = tr_half()
                                    for i, s_ in enumerate(srcs):
                                        op("pe", lambda e, s_=s_, i=i, ph=ph: e.matmul(ph[:, i * 128:(i + 1) * 128], s_, ident[:, :], start=True, stop=True), [zb.t, ident.t], [ptok], sig=(i == 3))
                                    cpa(dst, ph.rearrange("p (a b) -> p a b", a=4), [ptok], [zkc.t])
                                    cpa(dst2, ph.rearrange("p (a b) -> p a b", a=4), [ptok], [zks.t])
                        checkpoint('H1')
                        with scope():
                            modt = [P.sb(f"modt{i}", [128, 256], F32) for i in range(2)]
                            bm = [P.sb(f"bm{i}", [128, 256], F32) for i in range(2)]
                            sdt = [P.sb(f"sdt{i}", [128, 256], F32) for i in range(2)]
                            for jc in range(32):
                                ps = rotp()
                                lh = h2T[:, jc * 128:(jc + 1) * 128]
                                P.mm(ps[:, 0:256], lh, w3s[:, cg * 256:(cg + 1) * 256], True, True, [h2T.t, w3s.t], [ps.t], sig=False)
                                P.mm(ps[:, 256:512], lh, w3s[:, 512 + cg * 256:512 + (cg + 1) * 256], True, True, [h2T.t, w3s.t], [ps.t])
                                md, b_, s_ = modt[jc % 2], bm[jc % 2], sdt[jc % 2]
                                act(md[:, :], dlt[:, cg * 256:(cg + 1) * 256], AF.Exp, [dlt.t, tcol.t], [md.t], scale=tcol[:, jc:jc + 1])
                                ts(b_[:, :], ps[:, 256:512], mkb[:, jc:jc + 1], 0.0, ALU.mult, ALU.add, [ps.t, mkb.t], [b_.t])
                                tt(s_[:, :], ps[:, 0:256], b_[:, :], ALU.add, [ps.t, b_.t], [s_.t])
                                tt(zkc[:, jc, 256:512], s_[:, :], md[:, :], ALU.mult, [s_.t, md.t], [zkc.t])
                                tt(s_[:, :], ps[:, 0:256], b_[:, :], ALU.subtract, [ps.t, b_.t], [s_.t])
                                tt(zks[:, jc, 256:512], s_[:, :], md[:, :], ALU.mult, [s_.t, md.t], [zks.t])
                        checkpoint('H2')
                        with scope():
                            cst = [P.sb(f"cst{i}", [128, 4096], BF16) for i in range(2)]
                            sst = [P.sb(f"sst{i}", [128, 4096], BF16) for i in range(2)]
                            A32 = P.sb("A32", [128, 512], F32)
                            B32 = P.sb("B32", [128, 512], F32)
                            q1 = P.sb("q1", [128, 256], F32)
                            q2 = P.sb("q2", [128, 256], F32)
                            for kc in range(32):
                                c_, s_ = cst[kc % 2], sst[kc % 2]
                                ld(c_[:, :], Cf[kc], [c_.t]); ld(s_[:, :], Sf[kc], [s_.t])
                                pC, pS = rotp(), rotp()
                                for tc in range(32):
                                    f, la = tc == 0, tc == 31
                                    P.mm(pC[:, :], c_[:, tc * 128:(tc + 1) * 128], zkc[:, tc, :], f, la, [c_.t, zkc.t], [pC.t])
                                    P.mm(pS[:, :], s_[:, tc * 128:(tc + 1) * 128], zks[:, tc, :], f, la, [s_.t, zks.t], [pS.t])
                                cpa(A32[:, :], pC[:, :], [pC.t], [A32.t])
                                cpa(B32[:, :], pS[:, :], [pS.t], [B32.t])
                                tt(q1[:, :], A32[:, 0:256], A32[:, 256:512], ALU.mult, [A32.t], [q1.t])
                                tt(q2[:, :], B32[:, 0:256], B32[:, 256:512], ALU.mult, [B32.t], [q2.t])
                                tt(Pr[:, kc, :], q1[:, :], q2[:, :], ALU.subtract, [q1.t, q2.t], [Pr.t])
                                tt(q1[:, :], A32[:, 0:256], B32[:, 256:512], ALU.mult, [A32.t, B32.t], [q1.t])
                                tt(q2[:, :], B32[:, 0:256], A32[:, 256:512], ALU.mult, [A32.t, B32.t], [q2.t])
                                tt(Pi[:, kc, :], q1[:, :], q2[:, :], ALU.add, [q1.t, q2.t], [Pi.t])
                        checkpoint('H3')
                        with scope():
                            cip = [P.sb(f"cip{i}", [128, 4096], BF16) for i in range(2)]
                            sip = [P.sb(f"sip{i}", [128, 4096], BF16) for i in range(2)]
                            x0t = [P.sb(f"x0t{i}", [128, 512], BF16) for i in range(2)]
                            zt_ = [P.sb(f"zt_{i}", [128, 512], BF16) for i in range(2)]
                            e1 = [P.sb(f"e1{i}", [128, 512], F32) for i in range(2)]
                            yo = [P.sb(f"yo{i}", [128, 512], BF16) for i in range(2)]
                            pi_ = [0]
                            for tb in range(8):
                                pacc = [accb[0], accb[1]]
                                for piece in range(4):
                                    pi_[0] ^= 1
                                    c_, s_ = cip[pi_[0]], sip[pi_[0]]
                                    ld(c_[:, :], Ci[tb, piece], [c_.t]); ld(s_[:, :], Si[tb, piece], [s_.t])
                                    for kk in range(8):
                                        kc = piece * 8 + kk
                                        for cc in range(2):
                                            P.mm(pacc[cc][:, :], Pr[:, kc, cc * 128:(cc + 1) * 128], c_[:, kk * 512:(kk + 1) * 512], kc == 0, False, [Pr.t, c_.t], [pacc[cc].t], sig=False)
                                            P.mm(pacc[cc][:, :], Pi[:, kc, cc * 128:(cc + 1) * 128], s_[:, kk * 512:(kk + 1) * 512], False, kc == 31, [Pi.t, s_.t], [pacc[cc].t], sig=(kc == 31 or (kk == 7 and cc == 1)))
                                for cc in range(2):
                                    c4 = cg * 2 + cc
                                    rows = slice(c4 * 128, (c4 + 1) * 128)
                                    cols = slice(tb * 512, (tb + 1) * 512)
                                    ld(x0t[cc][:, :], x0cd[rows, cols], [x0t[cc].t]); ld(zt_[cc][:, :], zTd[rows, cols], [zt_[cc].t])
                                    ts(e1[cc][:, :], zt_[cc][:, :], hbs[:, c4:c4 + 1], 0.0, ALU.mult, ALU.add, [zt_[cc].t, hbs.t], [e1[cc].t])
                                    tt(e1[cc][:, :], pacc[cc][:, :], e1[cc][:, :], ALU.add, [pacc[cc].t, e1[cc].t], [e1[cc].t])
                                    tt(yo[cc][:, :], e1[cc][:, :], x0t[cc][:, :], ALU.mult, [e1[cc].t, x0t[cc].t], [yo[cc].t])
                                    st(yT3[rows, cols], yo[cc][:, :], [yo[cc].t])

            checkpoint('H')
            for mode in range(2):
                if mode == 1:
                    checkpoint('AT')
                with scope():
                    nkv = 2 if mode == 0 else 4
                    ksrc, vsrc, qsrc = (akd, avd, aqd) if mode == 0 else (rkd, rvd, rqd)
                    kT = P.sb("kT", [128, nkv, NT], BF16)
                    V = P.sb("V", [128, 32, nkv * 128], BF16)
                    ktl = [P.sb(f"ktl{i}", [128, nkv * 128], BF16) for i in range(2)]
                    qtl = [P.sb(f"qtl{i}", [128, 512], BF16) for i in range(2)]
                    qT = P.sb("qT", [128, 4, 512], BF16)
                    PT = [P.sb(f"PT{i}", [128, 512], BF16) for i in range(3)]
                    Dt = [P.sb(f"Dt{i}", [128, 512], F32) for i in range(2)]
                    fin = [P.sb(f"fin{i}", [128, 512], F32) for i in range(4)]
                    yo = [P.sb(f"yoA{i}", [128, 512], BF16) for i in range(2)]
                    rgt = P.sb("rgt", [128, 512], BF16)
                    bF = P.sb("bF", [128, 4, 256], F32)
                    bB = P.sb("bB", [128, 4, 256], F32)
                    djq = P.sb("djq", [128, 256], F32)
                    mjq = P.sb("mjq", [128, 256], F32)
                    base = P.sb("base", [128, 512], F32)
                    nlg = P.sb("nlg", [128, 8], F32)
                    o32f = P.sb("o32f", [128, 128], F32)
                    ld(V[:, :, :], vsrc[:, :].rearrange("(g p) c -> p g c", p=128), [V.t])
                    ld(djq[:, :], djq_in[:, :], [djq.t]); ld(mjq[:, :], mjq_all[:, mode * 256:(mode + 1) * 256], [mjq.t]); ld(base[:, :], base_in[:, :], [base.t])
                    mset(o32f[:, :], 1.0 / 128, [o32f.t])
                    ts(nlg[:, :], lgt[:, :], -1.0, 0.0, ALU.mult, ALU.add, [lgt.t], [nlg.t])
                    if mode == 1:
                        for h in range(4):
                            stt(bF[:, h, :], djq[:, :], lgt[:, h:h + 1], mjq[:, :], ALU.mult, ALU.add, [djq.t, lgt.t, mjq.t], [bF.t])
                            stt(bB[:, h, :], djq[:, :], nlg[:, 4 + h:5 + h], mjq[:, :], ALU.mult, ALU.add, [djq.t, nlg.t, mjq.t], [bB.t])
                        Dmix = P.sb("Dmix", [128, 16, 512], F32)
                        for h in range(4):
                            for di in range(4):
                                f0, f1 = fin[0], fin[1]
                                ts(f0[:, :], base[:, :], float(-128 * di), 0.0, ALU.add, ALU.max, [base.t], [f0.t])
                                ts(f1[:, :], base[:, :], float(-128 * di), 0.0, ALU.add, ALU.min, [base.t], [f1.t])
                                ts(f0[:, :], f0[:, :], lgt[:, h:h + 1], 0.0, ALU.mult, ALU.add, [f0.t, lgt.t], [f0.t])
                                stt(f0[:, :], f1[:, :], nlg[:, 4 + h:5 + h], f0[:, :], ALU.mult, ALU.add, [f1.t, nlg.t, f0.t], [f0.t])
                                act(Dmix[:, h * 4 + di, :], f0[:, :], AF.Exp, [f0.t, mjq.t], [Dmix.t], bias=mjq[:, 0:1])
                    for g in range(32):
                        kt_ = ktl[g % 2]
                        ld(kt_[:, :], ksrc[g * 128:(g + 1) * 128, :], [kt_.t])
                        ph, ptok = tr_half()
                        for h in range(nkv):
                            op("pe", lambda e, h=h, ph=ph, kt_=kt_: e.matmul(ph[:, h * 128:(h + 1) * 128], kt_[:, h * 128:(h + 1) * 128], ident[:, :], start=True, stop=True), [kt_.t, ident.t], [ptok], sig=(h == nkv - 1))
                        cpa(kT[:, :, g * 128:(g + 1) * 128], ph[:, 0:nkv * 128].rearrange("p (a b) -> p a b", a=nkv), [ptok], [kT.t])
                    pti = [0]
                    for qb in range(8):
                        for s in range(4):
                            qt_ = qtl[s % 2]
                            ld(qt_[:, :], qsrc[qb * 512 + s * 128:qb * 512 + (s + 1) * 128, :], [qt_.t])
                            ph, ptok = tr_half()
                            for h in range(4):
                                op("pe", lambda e, h=h, ph=ph, qt_=qt_: e.matmul(ph[:, h * 128:(h + 1) * 128], qt_[:, h * 128:(h + 1) * 128], ident[:, :], start=True, stop=True), [qt_.t, ident.t], [ptok], sig=(h == 3))
                            cpa(qT[:, :, s * 128:(s + 1) * 128], ph.rearrange("p (a b) -> p a b", a=4), [ptok], [qT.t])
                        for h in range(4):
                            kvh = h // 2 if mode == 0 else h
                            O, Dn = accb[0], accb[1]
                            for j in range(32):
                                ps = rotp()
                                P.mm(ps[:, :], kT[:, kvh, j * 128:(j + 1) * 128], qT[:, h, :], True, True, [kT.t, qT.t], [ps.t])
                                pti[0] = (pti[0] + 1) % 3
                                pt = PT[pti[0]]
                                jq = j * 8 + qb
                                if mode == 0:
                                    act(pt[:, :], ps[:, :], AF.Exp, [ps.t, mjq.t], [pt.t], scale=SC128, bias=mjq[:, jq:jq + 1])
                                else:
                                    dlt_ = 512 * qb - 128 * j
                                    dt__ = Dt[j % 2]
                                    if dlt_ >= 128:
                                        act(dt__[:, :], base[:, :], AF.Exp, [base.t, bF.t, lgt.t], [dt__.t], scale=lgt[:, h:h + 1], bias=bF[:, h, jq:jq + 1])
                                    elif dlt_ <= -512:
                                        act(dt__[:, :], base[:, :], AF.Exp, [base.t, bB.t, nlg.t], [dt__.t], scale=nlg[:, 4 + h:5 + h], bias=bB[:, h, jq:jq + 1])
                                    else:
                                        dmx = Dmix[:, h * 4 + (-dlt_) // 128, :]
                                    if -512 < dlt_ < 128:
                                        tt(pt[:, :], ps[:, :], dmx, ALU.mult, [ps.t, Dmix.t], [pt.t])
                                    else:
                                        tt(pt[:, :], ps[:, :], dt__[:, :], ALU.mult, [ps.t, dt__.t], [pt.t])
                                P.mm(O[:, :], V[:, j, kvh * 128:(kvh + 1) * 128], pt[:, :], j == 0, j == 31, [V.t, pt.t], [O.t], sig=(j == 31))
                                if mode == 0:
                                    P.mm(Dn[:, :], ones[:, :], pt[:, :], j == 0, j == 31, [ones.t, pt.t], [Dn.t], sig=(j == 31))
                            y_ = yo[h % 2]
                            cols = slice(qb * 512, (qb + 1) * 512)
                            if mode == 0:
                                recip(fin[0][:, :], Dn[:, :], [Dn.t], [fin[0].t])
                                tt(y_[:, :], O[:, :], fin[0][:, :], ALU.mult, [O.t, fin[0].t], [y_.t])
                                st(yT3[1024 + h * 128:1024 + (h + 1) * 128, cols], y_[:, :], [y_.t])
                            else:
                                o_, sq_, g_, g2_ = fin
                                cpa(o_[:, :], O[:, :], [O.t], [o_.t])
                                tt(sq_[:, :], o_[:, :], o_[:, :], ALU.mult, [o_.t], [sq_.t])
                                m1, m2 = rotp(), rotp()
                                P.mm(m1[:, :], o32f[:, :], o_[:, :], True, True, [o32f.t, o_.t], [m1.t])
                                P.mm(m2[:, :], o32f[:, :], sq_[:, :], True, True, [o32f.t, sq_.t], [m2.t])
                                tt(o_[:, :], o_[:, :], m1[:, :], ALU.subtract, [o_.t, m1.t], [o_.t])
                                act(sq_[:, :], m1[:, :], AF.Square, [m1.t], [sq_.t])
                                tt(sq_[:, :], m2[:, :], sq_[:, :], ALU.subtract, [m2.t, sq_.t], [sq_.t])
                                ts(sq_[:, :], sq_[:, :], EPS, 0.0, ALU.add, ALU.add, [sq_.t], [sq_.t])
                                act(sq_[:, :], sq_[:, :], AF.Ln, [sq_.t], [sq_.t])
                                act(sq_[:, :], sq_[:, :], AF.Exp, [sq_.t], [sq_.t], scale=-0.5)
                                tt(o_[:, :], o_[:, :], sq_[:, :], ALU.mult, [o_.t, sq_.t], [o_.t])
                                ld(rgt[:, :], rgT[h * 128:(h + 1) * 128, cols], [rgt.t])
                                act(g_[:, :], rgt[:, :], AF.Tanh, [rgt.t], [g_.t], scale=0.5)
                                ts(g_[:, :], g_[:, :], 0.5, 0.5, ALU.mult, ALU.add, [g_.t], [g_.t])
                                tt(g_[:, :], g_[:, :], rgt[:, :], ALU.mult, [g_.t, rgt.t], [g_.t])
                                tt(y_[:, :], o_[:, :], g_[:, :], ALU.mult, [o_.t, g_.t], [y_.t])
                                st(yT3[512 + h * 128:512 + (h + 1) * 128, cols], y_[:, :], [y_.t])
            checkpoint('RT')
            with scope():
                mt = [P.sb(f"mt{i}", [128, D], F32) for i in range(2)]
                junk = P.sb("junkM", [128, D], F32)
                hb = P.sb("hbM", [128, D], BF16)
                ss = P.sb("ssM", [128, 8], F32)
                memT = P.sb("memT", [128, 8, 512], BF16)
                ring = [P.sb(f"wrM{i}", [128, 8, 512], BF16) for i in range(2)]
                ridx = [0]
                for sl in range(2):
                    for mc in range(2):
                        m_ = mt[mc]
                        ld(m_[:, :], mem_in[sl, mc * 128:(mc + 1) * 128, :], [m_.t])
                        norm_T(m_[:, :], [m_.t], gp[:, 24:32], memT, sl * 256 + mc * 128, junk, hb, ss)
                for grp in range(4):
                    wb = wload(ring, ridx, wbf["w_xkv"][l], 0, 8, grp * 512, 512)
                    if grp < 2:
                        for f in range(4):
                            ps = rotp()
                            for kc in range(8):
                                P.mm(ps[:, :], wb[:, kc, f * 128:(f + 1) * 128], memT[:, kc, :], kc == 0, kc == 7, [wb.t, memT.t], [ps.t])
                            evac(kkT[:, grp * 4 + f, :], ps[:, :], [ps.t], [kkT.t])
                    else:
                        for sm in range(4):
                            ps = rotp()
                            for kc in range(8):
                                P.mm(ps[:, :], memT[:, kc, sm * 128:(sm + 1) * 128], wb[:, kc, :], kc == 0, kc == 7, [wb.t, memT.t], [ps.t])
                            evac(vv[:, sm, (grp - 2) * 512:(grp - 1) * 512], ps[:, :], [ps.t], [vv.t])

            rot_pool[0] = rot + [accb[0], accb[1]]
            with scope():
                gpo = [P.sb(f"gpo{i}", [128, D], F32) for i in range(3)]
                for i in range(3):
                    ld(gpo[i][:, :], gpost[l * 3 + i:l * 3 + i + 1, :].partition_broadcast(128), [gpo[i].t])
                xt = P.sb("xtE", [128, 4, D], F32)
                fmA = P.sb("fmA", [128, 8, 512], BF16)
                fmB = P.sb("fmB", [128, 8, 512], BF16)
                fmC = P.sb("fmC", [128, 8, 512], BF16)
                yTall = P.sb("yTall", [128, 12, 512], BF16)
                macc4 = P.sb("macc4", [128, 4, 512], F32)
                gt_ = [P.sb(f"gtE{i}", [128, 512], F32) for i in range(2)]
                osb = P.sb("osb", [128, 4, D], F32)
                junk = P.sb("junkE", [128, D], F32)
                hb = P.sb("hbE", [128, D], BF16)
                ss = P.sb("ssE", [128, 8], F32)
                PTx = [P.sb(f"PTx{i}", [128, 2, 512], BF16) for i in range(2)]
                uT = P.sb("uT", [128, 32, 512], BF16)
                rl = [P.sb(f"rl{i}", [128, 512], BF16) for i in range(2)]
                ring = [P.sb(f"wrE{i}", [128, 8, 512], BF16) for i in range(4)]
                ridx = [0]

                def post_norm_residual(lhs_fm, nfc, wsrc, gtile, T):
                    for ch in range(2):
                        for f0 in range(0, nfc, 8):
                            wb = wload(ring, ridx, wsrc, f0 * 128, 8, ch * 512, 512)
                            for s in range(4):
                                ps = quad[s]
                                for k_ in range(8):
                                    fc = f0 + k_
                                    P.mm(ps[:, :], lhs_fm[:, fc, s * 128:(s + 1) * 128], wb[:, k_, :], fc == 0, fc == nfc - 1, [lhs_fm.t, wb.t], [ps.t], sig=(fc == nfc - 1 or (k_ == 7 and s == 3)))
                        for s in range(4):
                            evac(osb[:, s, ch * 512:(ch + 1) * 512], quad[s][:, :], [quad[s].t], [osb.t])
                    for s in range(4):
                        mset(ss[:, 0:1], 0.0, [ss.t])
                        act(junk[:, :], osb[:, s, :], AF.Square, [osb.t], [junk.t, ss.t], accum_out=ss[:, 0:1])
                        rstd_from_ss(ss, 1, 1.0 / D, [ss.t])
                        tt(osb[:, s, :], osb[:, s, :], gtile[:, :], ALU.mult, [osb.t, gtile.t], [osb.t])
                        stt(xt[:, s, :], osb[:, s, :], ss[:, 0:1], xt[:, s, :], ALU.mult, ALU.add, [osb.t, ss.t, xt.t], [xt.t])

                wcache = {}
                for T in range(8):
                    sl = T // 4
                    for s in range(4):
                        ld(xt[:, s, :], xsrc[T * 512 + s * 128:T * 512 + (s + 1) * 128, :], [xt.t])
                    ld(fmA[:, :, :], hTd[:, :, T * 512:(T + 1) * 512].rearrange("kc p t -> p kc t"), [fmA.t])
                    ld(yTall[:, :, :], yT3[:, T * 512:(T + 1) * 512].rearrange("(fc p) t -> p fc t", p=128), [yTall.t])
                    for dc4 in range(2):
                        for br in range(3):
                            wg = wload(ring, ridx, wbf["w_in"][l], 0, 8, 4608 + br * 1024 + dc4 * 512, 512)
                            wbr = wload(ring, ridx, wbf["w_br"][l], br * 512, 4, dc4 * 512, 512)
                            for dl in range(4):
                                dc = dc4 * 4 + dl
                                pg = rotp()
                                for kc in range(8):
                                    P.mm(pg[:, :], wg[:, kc, dl * 128:(dl + 1) * 128], fmA[:, kc, :], kc == 0, kc == 7, [wg.t, fmA.t], [pg.t])
                                pp = rotp()
                                for fc in range(4):
                                    P.mm(pp[:, :], wbr[:, fc, dl * 128:(dl + 1) * 128], yTall[:, br * 4 + fc, :], fc == 0, fc == 3, [wbr.t, yTall.t], [pp.t])
                                g_ = gt_[dl % 2]
                                act(g_[:, :], pg[:, :], AF.Tanh, [pg.t], [g_.t], scale=0.5)
                                ts(g_[:, :], g_[:, :], 0.5, 0.5, ALU.mult, ALU.add, [g_.t], [g_.t])
                                if br == 0:
                                    tt(macc4[:, dl, :], pp[:, :], g_[:, :], ALU.mult, [pp.t, g_.t], [macc4.t])
                                else:
                                    tt(g_[:, :], pp[:, :], g_[:, :], ALU.mult, [pp.t, g_.t], [g_.t])
                                    if br == 1:
                                        tt(macc4[:, dl, :], macc4[:, dl, :], g_[:, :], ALU.add, [macc4.t, g_.t], [macc4.t])
                                    else:
                                        tt(fmB[:, dc, :], macc4[:, dl, :], g_[:, :], ALU.add, [macc4.t, g_.t], [fmB.t])
                    post_norm_residual(fmB, 8, wbf["w_out"][l], gpo[0], T)
                    for s in range(4):
                        norm_T(xt[:, s, :], [xt.t], gp[:, 8:16], fmA, s * 128, junk, hb, ss)
                    for fg in range(2):
                        wb = wload(ring, ridx, wbf["w_xq"][l], 0, 8, fg * 512, 512)
                        for f in range(4):
                            ps = rotp()
                            for kc in range(8):
                                P.mm(ps[:, :], wb[:, kc, f * 128:(f + 1) * 128], fmA[:, kc, :], kc == 0, kc == 7, [wb.t, fmA.t], [ps.t])
                            evac(fmB[:, fg * 4 + f, :], ps[:, :], [ps.t], [fmB.t])
                    for h in range(4):
                        ptx = PTx[h % 2]
                        for mc in range(2):
                            ps = rotp()
                            for c2 in range(2):
                                P.mm(ps[:, :], kkT[:, 2 * h + c2, sl * 256 + mc * 128:sl * 256 + (mc + 1) * 128], fmB[:, 2 * h + c2, :], c2 == 0, c2 == 1, [kkT.t, fmB.t], [ps.t])
                            act(ptx[:, mc, :], ps[:, :], AF.Exp, [ps.t], [ptx.t], scale=1.0 / 16.0)
                        pd = rotp()
                        for mc in range(2):
                            P.mm(pd[:, :], ones[:, :], ptx[:, mc, :], mc == 0, mc == 1, [ones.t, ptx.t], [pd.t])
                        g_ = gt_[h % 2]
                        recip(g_[:, :], pd[:, :], [pd.t], [g_.t])
                        for c2 in range(2):
                            po = rotp()
                            for mc in range(2):
                                P.mm(po[:, :], vv[:, sl * 2 + mc, (2 * h + c2) * 128:(2 * h + c2 + 1) * 128], ptx[:, mc, :], mc == 0, mc == 1, [vv.t, ptx.t], [po.t])
                            tt(fmC[:, 2 * h + c2, :], po[:, :], g_[:, :], ALU.mult, [po.t, g_.t], [fmC.t])
                    post_norm_residual(fmC, 8, wbf["w_xo"][l], gpo[1], T)
                    for s in range(4):
                        norm_T(xt[:, s, :], [xt.t], gp[:, 16:24], fmA, s * 128, junk, hb, ss)
                    for fg in range(8):
                        wb = wload(ring, ridx, wbf["w_ff1"][l], 0, 8, fg * 512, 512)
                        for f in range(4):
                            ps = rotp()
                            for kc in range(8):
                                P.mm(ps[:, :], wb[:, kc, f * 128:(f + 1) * 128], fmA[:, kc, :], kc == 0, kc == 7, [wb.t, fmA.t], [ps.t])
                            r_ = rl[f % 2]
                            act(r_[:, :], ps[:, :], AF.Relu, [ps.t], [r_.t])
                            tt(uT[:, fg * 4 + f, :], r_[:, :], r_[:, :], ALU.mult, [r_.t], [uT.t])
                    post_norm_residual(uT, 32, wbf["w_ff2"][l], gpo[2], T)
                    for s in range(4):
                        st(xdst[T * 512 + s * 128:T * 512 + (s + 1) * 128, :], xt[:, s, :], [xt.t])
    except _Stop:
        pass
    P.finish()
    return nc


def _core_tables(L):
    import ml_dtypes
    nseq = NT // L
    tl = np.arange(NT) % L
    t = {}
    inv = 10000.0 ** (-np.arange(0, 128, 2, dtype=np.float32) / 128)
    ang = tl.astype(np.float32)[:, None] * inv[None, :]
    t["cosR"], t["sinR"] = np.cos(ang).astype(np.float32), np.sin(ang).astype(np.float32)
    inv2 = 10000.0 ** (-np.arange(0, 64, 2, dtype=np.float32) / 64)
    ar = (tl // 64).astype(np.float32)[:, None] * inv2[None, :]
    ac = (tl % 64).astype(np.float32)[:, None] * inv2[None, :]
    t["axc"] = np.concatenate([np.cos(ar), np.cos(ac)], 1).astype(np.float32)
    t["axs"] = np.concatenate([np.sin(ar), np.sin(ac)], 1).astype(np.float32)
    tt_ = np.linspace(0.0, 1.0, L, dtype=np.float32)[:, None]
    bands = 16
    w = (2.0 * math.pi * np.arange(L, dtype=np.float32)[:, None] / L).astype(np.float32)
    f = np.linspace(1e-4, bands - 1, bands, dtype=np.float32)[None, :]
    z = np.concatenate([tt_, np.cos(f * w), -np.sin(f * w)], -1).astype(np.float32)
    z = np.tile(z, (nseq, 1))
    t["zemb"] = np.ascontiguousarray(z.T)
    tj = np.tile(tt_[:, 0], nseq)
    t["tcol"] = np.ascontiguousarray(tj.reshape(32, 128).T)
    mk = np.ones(NT, np.float32); mk[tl == 0] = 0.0
    t["maskb"] = np.ascontiguousarray(mk.reshape(32, 128).T)
    min_decay = math.log(1e-2) / 1.5
    max_decay = math.log(1e-2) / 0.3
    t["delta"] = np.abs(np.linspace(min_decay, max_decay, 512, dtype=np.float32))[None, :].astype(np.float32)
    misc = np.zeros((128, 8), np.float32)
    misc[:, 0] = 1.0 if nseq == 1 else 0.0
    t["misc"] = misc
    jj, qq = np.meshgrid(np.arange(32), np.arange(8), indexing="ij")
    djq = (512 * qq - 128 * jj).astype(np.float32).reshape(-1)
    t["djq"] = np.tile(djq[None, :], (128, 1)).astype(np.float32)
    cross = ((jj * 128) // L != (qq * 512) // L).reshape(-1)
    mb = np.where(cross, -30000.0, 0.0).astype(np.float32)
    mjq = np.concatenate([mb, mb + math.log(SC128)])[None, :]
    t["mjq"] = np.tile(mjq, (128, 1)).astype(np.float32)
    t["base"] = (np.arange(512, dtype=np.float32)[None, :] - np.arange(128, dtype=np.float32)[:, None]).astype(np.float32)
    N = 2 * L
    k = np.arange(L, dtype=np.float64)
    n = np.arange(L, dtype=np.float64)
    th = np.pi * np.outer(n, 2 * k + 1) / N
    C1, S1 = np.cos(th), np.sin(th)
    Cfull = np.zeros((NT, NT), np.float32); Sfull = np.zeros((NT, NT), np.float32)
    for s_ in range(nseq):
        Cfull[s_ * L:(s_ + 1) * L, s_ * L:(s_ + 1) * L] = C1
        Sfull[s_ * L:(s_ + 1) * L, s_ * L:(s_ + 1) * L] = S1
    bf = ml_dtypes.bfloat16
    def fwd(M):
        return np.ascontiguousarray(M.reshape(32, 128, 32, 128).transpose(2, 1, 0, 3).reshape(32, 128, 4096)).astype(bf)
    t["Cf"], t["Sf"] = fwd(Cfull), fwd(Sfull)
    def invm(M):
        Mi = (M.T * (2.0 / N)).astype(np.float32)
        return np.ascontiguousarray(Mi.reshape(4, 8, 128, 8, 512).transpose(3, 0, 2, 1, 4).reshape(8, 4, 128, 4096)).astype(bf)
    t["Ci"], t["Si"] = invm(Cfull), invm(Sfull)
    return t


def _shared_tables():
    p = np.arange(128, dtype=np.float32)
    d = p[None, :] - p[:, None]
    ct = np.zeros((128, 6 * 128 + 8), np.float32)
    ct[:, 0:128] = np.maximum(d, 0); ct[:, 128:256] = np.maximum(-d, 0)
    ct[:, 256:384] = (d >= 0); ct[:, 384:512] = (d < 0)
    ct[:, 512:640] = p[None, :] + 1; ct[:, 640:768] = 128 - p[None, :]
    ct[:, 768] = p; ct[:, 769] = 127 - p; ct[:, 770] = 128.0
    return {"ctab": ct, "ident": np.eye(128, dtype=np.float32)}


_NC_CACHE = {}


def make_in_maps(inputs, depth, cores=range(8)):
    f32 = lambda a: np.ascontiguousarray(np.asarray(a, dtype=np.float32))
    inp = {k: f32(v) for k, v in inputs.items()}
    shared = {k: inp[k] for k in ("w_in", "w_branch", "w_out", "w_xq", "w_xkv", "w_xo", "w_ff1", "w_ff2",
                                  "hy_fw1", "hy_fw2", "hy_fw3", "att_qnorm", "att_knorm")}
    shared["convw"] = np.ascontiguousarray(inp["hy_conv"].reshape(depth, 3, 12, 128).transpose(3, 0, 1, 2).reshape(128, depth * 36))
    shared["fb1c"] = np.ascontiguousarray(inp["hy_fb1"].T)
    shared["fb2c"] = np.ascontiguousarray(inp["hy_fb2"].T)
    shared["hbiasc"] = np.ascontiguousarray(inp["hy_bias"].reshape(depth, 4, 128).transpose(2, 0, 1).reshape(128, depth * 4))
    shared["rdec"] = np.ascontiguousarray(inp["ret_decay"].reshape(depth, 8))
    gcols = lambda g: g.reshape(depth, 8, 128).transpose(2, 0, 1)
    shared["gpre"] = np.ascontiguousarray(np.concatenate([gcols(inp["g_mix_pre"]), gcols(inp["g_x_pre"]), gcols(inp["g_ff_pre"]), gcols(inp["g_mem"])], axis=2).reshape(128, depth * 32))
    shared["gpost"] = np.ascontiguousarray(np.stack([inp["g_mix_post"], inp["g_x_post"], inp["g_ff_post"]], axis=1).reshape(depth * 3, D))
    shared.update(_shared_tables())
    tabs = {4096: _core_tables(4096), 2048: _core_tables(2048)}
    in_maps = []
    for c in cores:
        m = dict(shared)
        if c < 4:
            m["x"] = inp["x_prompt"][c]
            m["mem"] = np.ascontiguousarray(np.stack([inp["mem_prompt"][c], inp["mem_prompt"][c]]))
            m.update(tabs[4096])
        else:
            b0 = 2 * (c - 4)
            m["x"] = np.ascontiguousarray(inp["x_sample"][b0:b0 + 2].reshape(NT, D))
            m["mem"] = np.ascontiguousarray(inp["mem_sample"][b0:b0 + 2])
            m.update(tabs[2048])
        in_maps.append(m)
    return in_maps


def kernel(**inputs):
    depth = L_DEPTH
    if "nc" not in _NC_CACHE:
        _NC_CACHE["nc"] = build_program(depth)
    nc = _NC_CACHE["nc"]
    in_maps = make_in_maps(inputs, depth)
    res = run_bass_kernel_spmd(nc, in_maps, core_ids=list(range(8)))
    outs = [np.asarray(r["y"], dtype=np.float32) for r in res.results]
    y_prompt = np.stack(outs[0:4]).reshape(4, 4096, D)
    y_sample = np.concatenate([o.reshape(2, 2048, D) for o in outs[4:8]], axis=0)
    return (y_prompt, y_sample)
```

```python
import contextlib
import math
import numpy as np
import concourse.bass as bass
import concourse.mybir as mybir
from concourse.bass_utils import run_bass_kernel_spmd

F32 = mybir.dt.float32
BF16 = mybir.dt.bfloat16
AF = mybir.ActivationFunctionType
ALU = mybir.AluOpType
AX = mybir.AxisListType


class Tok:
    __slots__ = ("w", "r", "name", "owner")

    def __init__(self, name="", owner=None):
        self.w = None
        self.r = []
        self.name = name
        self.owner = owner


class Buf:
    def __init__(self, handle, name, is_dram=False):
        self.h = handle
        self.name = name
        self.t = Tok(name, self)
        self.dsem = {}
        self.subs = {}
        self.is_dram = is_dram
        self._ap = handle.ap() if is_dram else None

    def __getitem__(self, key):
        if self.is_dram:
            return self._ap[key]
        return self.h[key]

    def tok(self, i):
        if i not in self.subs:
            self.subs[i] = Tok(f"{self.name}.{i}", self)
        return self.subs[i]


class Prog:
    ENG = ("pe", "act", "dve", "pool", "sp")

    def __init__(self, nc):
        self.nc = nc
        self.stack = contextlib.ExitStack()
        self.ops = {e: [] for e in self.ENG}
        self.NDS = 40
        self.streams = list(self.ENG) + [f"ds{i}" for i in range(self.NDS)]
        self.free_ds = {"sp": [f"ds{i}" for i in range(0, 26)], "pool": [f"ds{i}" for i in range(26, self.NDS)]}
        self.scope_bufs = [[]]
        self.sem = {}
        for s in self.streams:
            self.sem[s] = self.stack.enter_context(nc.semaphore("sem_" + s))
        self.cnt = {s: 0 for s in self.streams}
        self.seen = {e: {s: 0 for s in self.streams} for e in self.ENG}
        self.nops = 0
        self.nwaits = 0

    def dram_in(self, name, shape, dt):
        return Buf(self.nc.dram_tensor(name, list(shape), dt, kind="ExternalInput"), name, True)

    def dram_out(self, name, shape, dt):
        return Buf(self.nc.dram_tensor(name, list(shape), dt, kind="ExternalOutput"), name, True)

    def dram_tmp(self, name, shape, dt):
        return Buf(self.nc.dram_tensor(name, list(shape), dt), name, True)

    def sb(self, name, shape, dt):
        self.uid = getattr(self, "uid", 0) + 1
        name = f"s{self.uid}_{name}"
        b = self._sb(name, shape, dt)
        self.scope_bufs[-1].append(b)
        return b

    def _sb(self, name, shape, dt):
        return Buf(self.stack.enter_context(self.nc.sbuf_tensor(name, list(shape), dt)), name)

    def ps(self, name, shape, dt):
        name = "p_" + name
        return Buf(self.stack.enter_context(self.nc.psum_tensor(name, list(shape), dt)), name)

    def op(self, eng, fn, reads=(), writes=(), sig=True, dma=False):
        if getattr(self, 'maxops', None) and self.nops >= self.maxops and sig:
            raise _Stop()
        if dma:
            owner = None
            for t in list(writes) + list(reads):
                if t.owner is not None and not t.owner.is_dram:
                    owner = t.owner
                    break
            assert owner is not None, "DMA needs an SBUF-side token"
            qk = "pool" if eng == "pool" else "sp"
            if qk not in owner.dsem:
                owner.dsem[qk] = self.free_ds[qk].pop(0)
            stream = owner.dsem[qk]
        else:
            stream = eng
        deps = {}
        for t in reads:
            if t.w is not None:
                s, c = t.w
                deps[s] = max(deps.get(s, 0), c)
        for t in writes:
            if t.w is not None:
                s, c = t.w
                deps[s] = max(deps.get(s, 0), c)
            for (s, c) in t.r:
                deps[s] = max(deps.get(s, 0), c)
        for s, c in deps.items():
            if s == "pe" and eng == "pe":
                continue
            if self.seen[eng][s] >= c:
                continue
            self.seen[eng][s] = c
            sem = self.sem[s]
            self.ops[eng].append(lambda e, sem=sem, c=c: e.wait_ge(sem, c))
            self.nwaits += 1
        if sig:
            self.cnt[stream] += 1 if not dma else 16
            myc = self.cnt[stream]
            sem = self.sem[stream]
            inc = 16 if dma else 1
            self.ops[eng].append(lambda e, fn=fn, sem=sem, inc=inc: fn(e).then_inc(sem, inc))
        else:
            myc = self.cnt[stream] + (16 if dma else 1)
            self.ops[eng].append(lambda e, fn=fn: fn(e))
        for t in reads:
            t.r.append((stream, myc))
        for t in writes:
            t.w = (stream, myc)
            t.r = []
        self.nops += 1

    def dma(self, eng, out_ap, in_ap, reads, writes, **kw):
        self.op(eng, lambda e: e.dma_start(out=out_ap, in_=in_ap, **kw), reads, writes, dma=True)

    def mm(self, out_ap, lhsT, rhs, start, stop, reads, writes, sig=None):
        if sig is None:
            sig = stop
        self.op("pe", lambda e: e.matmul(out_ap, lhsT, rhs, start=start, stop=stop), reads, writes, sig=sig)

    def finish(self):
        nc = self.nc
        for s in self.streams:
            if self.cnt[s] > 0 and s != "sp":
                c = self.cnt[s]
                sem = self.sem[s]
                self.ops["sp"].append(lambda e, sem=sem, c=c: e.wait_ge(sem, c))
        ops = self.ops
        with nc.Block() as block:
            @block.tensor
            def _(e):
                for f in ops["pe"]:
                    f(e)

            @block.scalar
            def _(e):
                for f in ops["act"]:
                    f(e)

            @block.vector
            def _(e):
                for f in ops["dve"]:
                    f(e)

            @block.gpsimd
            def _(e):
                for f in ops["pool"]:
                    f(e)

            @block.sync
            def _(e):
                for f in ops["sp"]:
                    f(e)
        self.stack.close()


L_DEPTH = 4
DBG = {}
NT = 4096
D = 1024
PI = math.pi
EPS = 1e-6
SC128 = 128.0 ** -0.5


class _Stop(Exception):
    pass


def build_program(depth=L_DEPTH, dbg=False, stop_after=None):
    nc = bass.Bass("TRN2", target_bir_lowering=False)
    P = Prog(nc)
    P.maxops = DBG.get('maxops')
    op = P.op

    def checkpoint(name):
        if stop_after == name:
            raise _Stop()

    def act(out, in_, func, R, W, **kw):
        op("act", lambda e: e.activation(out, in_, func, **kw), R, W)

    def tt(out, a, b, o, R, W):
        op("dve", lambda e: e.tensor_tensor(out, a, b, o), R, W)

    def ts(out, a, s1, s2, o0, o1, R, W):
        op("dve", lambda e: e.tensor_scalar(out, a, s1, s2, o0, o1), R, W)

    def stt(out, a, s, b, o0, o1, R, W):
        op("dve", lambda e: e.scalar_tensor_tensor(out, a, s, b, o0, o1), R, W)

    def mset(out, val, W):
        op("dve", lambda e: e.memset(out, val), [], W)

    def recip(out, in_, R, W):
        op("dve", lambda e: e.reciprocal(out, in_), R, W)

    def cpv(out, in_, R, W):
        op("dve", lambda e: e.tensor_copy(out, in_), R, W)

    def cpa(out, in_, R, W):
        op("act", lambda e: e.activation(out, in_, AF.Identity), R, W)

    evs = [0]

    def evac(out, in_, R, W):
        evs[0] ^= 1
        (cpa if evs[0] else cpv)(out, in_, R, W)

    def ld(out, in_, W, eng="sp"):
        P.dma(eng, out, in_, [], W)

    def st(out, in_, R, eng="pool"):
        P.dma(eng, out, in_, R, [])

    def barrier():
        for e in P.ENG:
            for s in P.streams:
                c = P.cnt[s]
                if c > P.seen[e][s]:
                    P.seen[e][s] = c
                    sem = P.sem[s]
                    P.ops[e].append(lambda en, sem=sem, c=c: en.wait_ge(sem, c))

    class scope:
        def __enter__(self):
            self.saved = P.stack
            P.stack = contextlib.ExitStack()
            P.scope_bufs.append([])
            return self

        def __exit__(self, *a):
            barrier()
            for b in P.scope_bufs.pop():
                for qk, ds in b.dsem.items():
                    P.free_ds[qk].append(ds)
                b.dsem = {}
            P.stack.close()
            P.stack = self.saved
            return False

    di = P.dram_in
    x_in = di("x", [NT, D], F32)
    mem_in = di("mem", [2, 256, D], F32)
    w_in = di("w_in", [depth, D, 7680], F32)
    w_br = di("w_branch", [depth, 1536, D], F32)
    w_out = di("w_out", [depth, D, D], F32)
    w_xq = di("w_xq", [depth, D, D], F32)
    w_xkv = di("w_xkv", [depth, D, 2 * D], F32)
    w_xo = di("w_xo", [depth, D, D], F32)
    w_ff1 = di("w_ff1", [depth, D, 4 * D], F32)
    w_ff2 = di("w_ff2", [depth, 4 * D, D], F32)
    convw = di("convw", [128, depth * 36], F32)
    fw1 = di("hy_fw1", [depth, 33, 64], F32)
    fb1 = di("fb1c", [64, depth], F32)
    fw2 = di("hy_fw2", [depth, 64, 64], F32)
    fb2 = di("fb2c", [64, depth], F32)
    fw3 = di("hy_fw3", [depth, 64, 1024], F32)
    hbias = di("hbiasc", [128, depth * 4], F32)
    rdec = di("rdec", [depth, 8], F32)
    qn_in = di("att_qnorm", [depth, 128], F32)
    kn_in = di("att_knorm", [depth, 128], F32)
    gpre = di("gpre", [128, depth * 32], F32)
    gpost = di("gpost", [depth * 3, D], F32)
    ident_in = di("ident", [128, 128], F32)
    ctab_in = di("ctab", [128, 6 * 128 + 8], F32)
    cosR = di("cosR", [NT, 64], F32)
    sinR = di("sinR", [NT, 64], F32)
    axc = di("axc", [NT, 64], F32)
    axs = di("axs", [NT, 64], F32)
    zemb = di("zemb", [33, NT], F32)
    tcol_in = di("tcol", [128, 32], F32)
    maskb_in = di("maskb", [128, 32], F32)
    delta_in = di("delta", [1, 512], F32)
    misc_in = di("misc", [128, 8], F32)
    djq_in = di("djq", [128, 256], F32)
    mjq_all = di("mjq", [128, 512], F32)
    base_in = di("base", [128, 512], F32)
    Cf = di("Cf", [32, 128, 32 * 128], BF16)
    Sf = di("Sf", [32, 128, 32 * 128], BF16)
    Ci = di("Ci", [8, 4, 128, 8 * 512], BF16)
    Si = di("Si", [8, 4, 128, 8 * 512], BF16)
    y_out = P.dram_out("y", [NT, D], F32)

    dt_ = P.dram_out if dbg else P.dram_tmp
    xres = dt_("xres", [NT, D], F32)
    hTd = dt_("hTd", [8, 128, NT], BF16)
    hyTd = dt_("hyTd", [1536, NT], BF16)
    rqd = dt_("rqd", [NT, 512], BF16)
    rkd = dt_("rkd", [NT, 512], BF16)
    rvd = dt_("rvd", [NT, 512], BF16)
    rgT = dt_("rgT", [512, NT], BF16)
    x0cd = dt_("x0cd", [512, NT], BF16)
    zTd = dt_("zTd", [512, NT], BF16)
    aqd = dt_("aqd", [NT, 512], BF16)
    akd = dt_("akd", [NT, 256], BF16)
    avd = dt_("avd", [NT, 256], BF16)
    yT3 = dt_("yT3", [1536, NT], BF16)
    ident = P.sb("ident", [128, 128], BF16)
    ones = P.sb("ones", [128, 128], BF16)
    ctab = P.sb("ctab", [128, 6 * 128 + 8], F32)
    misc = P.sb("misc", [128, 8], F32)
    rot = [P.ps(f"rot{i}", [128, 512], F32) for i in range(3)]
    accb = [P.ps(f"acc{i}", [128, 512], F32) for i in range(3)]
    trbs = [P.ps(f"trb{i}", [128, 512], F32) for i in range(2)]
    quad = [rot[0], rot[1], rot[2], accb[2]]
    roti = [0]

    rot_pool = [list(rot)]

    def rotp():
        roti[0] = (roti[0] + 1) % len(rot_pool[0])
        return rot_pool[0][roti[0]]

    tri = [DBG.get('tri0', 0)]

    def tr_half():
        tri[0] ^= 1
        h = tri[0]
        return trbs[h][:, :], trbs[h].t

    ld(ident[:, :], ident_in[:, :], [ident.t], eng="pool")
    ld(ctab[:, :], ctab_in[:, :], [ctab.t])
    ld(misc[:, :], misc_in[:, :], [misc.t])
    mset(ones[:, :], 1.0, [ones.t])
    R1, R2, M1, M2, NP1, NM = [ctab[:, i * 128:(i + 1) * 128] for i in range(6)]
    pcol = ctab[:, 768:769]
    pcol_r = ctab[:, 769:770]
    c128 = ctab[:, 770:771]
    mcoef = misc[:, 0:1]

    def transposes(srcs, R, dst, W, scale_col=None):
        n = len(srcs)
        ph, ptok = tr_half()
        for i, s_ in enumerate(srcs):
            op("pe", lambda e, s_=s_, i=i: e.matmul(ph[:, i * 128:(i + 1) * 128], s_, ident[:, :], start=True, stop=True),
               list(R) + [ident.t], [ptok], sig=(i == n - 1))
        if scale_col is None:
            evac(dst, ph[:, 0:n * 128], [ptok], W)
        else:
            ts(dst, ph[:, 0:n * 128], scale_col, 0.0, ALU.mult, ALU.add, [ptok], W)

    def rstd_from_ss(ss, n, inv_n, R_W):
        ts(ss[:, 0:n], ss[:, 0:n], inv_n, EPS, ALU.mult, ALU.add, R_W, R_W)
        act(ss[:, 0:n], ss[:, 0:n], AF.Ln, R_W, R_W)
        act(ss[:, 0:n], ss[:, 0:n], AF.Exp, R_W, R_W, scale=-0.5)

    gp_tok = [None]

    def norm_T(xt_ap, xtoks, gcol, hT, col0, junk, hb, ss):
        mset(ss[:, 0:1], 0.0, [ss.t])
        act(junk[:, :], xt_ap, AF.Square, xtoks, [junk.t, ss.t], accum_out=ss[:, 0:1])
        rstd_from_ss(ss, 1, 1.0 / D, [ss.t])
        ts(hb[:, :], xt_ap, ss[:, 0:1], 0.0, ALU.mult, ALU.add, xtoks + [ss.t], [hb.t])
        for half in range(2):
            ph, ptok = tr_half()
            for i in range(4):
                kc = half * 4 + i
                op("pe", lambda e, kc=kc, i=i, ph=ph: e.matmul(ph[:, i * 128:(i + 1) * 128], hb[:, kc * 128:(kc + 1) * 128], ident[:, :], start=True, stop=True),
                   [hb.t, ident.t], [ptok], sig=(i == 3))
            for i in range(4):
                kc = half * 4 + i
                ts(hT[:, kc, col0:col0 + 128], ph[:, i * 128:(i + 1) * 128], gcol[:, kc:kc + 1], 0.0, ALU.mult, ALU.add,
                   [ptok, gp_tok[0]], [hT.t])

    def bcast_mid(buf, off, F, inner, H):
        return bass.AP(buf.h, off, [[F, 128], [0, H]] + inner)

    def rotary(src, H, A, Wd, ctb, stb, toff, TF, dst_ap, dtok, t1, t2):
        xv = src[:, 0:H * 128].rearrange("p (h a two w) -> p h a two w", h=H, a=A, two=2, w=Wd)
        dv = dst_ap.rearrange("p (h a two w) -> p h a two w", h=H, a=A, two=2, w=Wd)
        x1, x2 = xv[:, :, :, 0, :], xv[:, :, :, 1, :]
        c = bcast_mid(ctb, toff, TF, [[Wd, A], [1, Wd]], H)
        s = bcast_mid(stb, toff, TF, [[Wd, A], [1, Wd]], H)
        a1 = t1[:, 0:H * 64].rearrange("p (h a w) -> p h a w", h=H, a=A, w=Wd)
        a2 = t2[:, 0:H * 64].rearrange("p (h a w) -> p h a w", h=H, a=A, w=Wd)
        tb = [ctb.t, stb.t]
        tt(a1, x1, c, ALU.mult, [src.t] + tb, [t1.t])
        tt(a2, x2, s, ALU.mult, [src.t] + tb, [t2.t])
        tt(dv[:, :, :, 0, :], a1, a2, ALU.subtract, [t1.t, t2.t], [dtok])
        tt(a1, x1, s, ALU.mult, [src.t] + tb, [t1.t])
        tt(a2, x2, c, ALU.mult, [src.t] + tb, [t2.t])
        tt(dv[:, :, :, 1, :], a1, a2, ALU.add, [t1.t, t2.t], [dtok])

    def wload(ring, ridx, wsrc2d, r0, nkc, c0, ncols):
        ridx[0] = (ridx[0] + 1) % len(ring)
        wb = ring[ridx[0]]
        src = wsrc2d[r0:r0 + nkc * 128, c0:c0 + ncols].rearrange("(kc p) c -> p kc c", p=128)
        P.dma("sp", wb[:, 0:nkc, 0:ncols], src, [], [wb.t])
        return wb


    wspec = [("w_in", w_in, D, 7680), ("w_br", w_br, 1536, D), ("w_out", w_out, D, D), ("w_xq", w_xq, D, D),
             ("w_xkv", w_xkv, D, 2 * D), ("w_xo", w_xo, D, D), ("w_ff1", w_ff1, D, 4 * D), ("w_ff2", w_ff2, 4 * D, D)]
    wbf = {}
    for nm, src_, K_, N_ in wspec:
        wbf[nm] = P.dram_tmp("bf_" + nm, [depth, K_, N_], BF16)
    with scope():
        stg32 = [P.sb(f"stg32_{i}", [128, 8192], F32) for i in range(2)]
        stg16 = [P.sb(f"stg16_{i}", [128, 8192], BF16) for i in range(2)]
        ci = [0]
        for nm, src_, K_, N_ in wspec:
            nkc_all = K_ // 128
            per = max(1, 8192 // N_)
            for l in range(depth):
                for k0 in range(0, nkc_all, per):
                    nk = min(per, nkc_all - k0)
                    ci[0] ^= 1
                    a32, a16 = stg32[ci[0]], stg16[ci[0]]
                    v32 = a32[:, 0:nk * N_].rearrange("p (k n) -> p k n", k=nk)
                    v16 = a16[:, 0:nk * N_].rearrange("p (k n) -> p k n", k=nk)
                    ld(v32, src_[l][k0 * 128:(k0 + nk) * 128, :].rearrange("(k p) n -> p k n", p=128), [a32.t])
                    if ci[0]:
                        cpa(a16[:, 0:nk * N_], a32[:, 0:nk * N_], [a32.t], [a16.t])
                    else:
                        cpv(a16[:, 0:nk * N_], a32[:, 0:nk * N_], [a32.t], [a16.t])
                    st(wbf[nm][l][k0 * 128:(k0 + nk) * 128, :].rearrange("(k p) n -> p k n", p=128), v16, [a16.t])

    try:
      for l in range(depth):
        xsrc = x_in if l == 0 else xres
        xdst = y_out if l == depth - 1 else xres
        with scope():
            gp = P.sb("gp", [128, 32], F32)
            ld(gp[:, :], gpre[:, l * 32:(l + 1) * 32], [gp.t])
            gp_tok[0] = gp.t
            lgt = P.sb("lgt", [128, 8], F32)
            ld(lgt[:, :], rdec[l:l + 1, :].partition_broadcast(128), [lgt.t])
            act(lgt[:, :], lgt[:, :], AF.Exp, [lgt.t], [lgt.t], scale=-1.0)
            act(lgt[:, :], lgt[:, :], AF.Ln, [lgt.t], [lgt.t], bias=1.0)
            ts(lgt[:, :], lgt[:, :], -1.0, 0.0, ALU.mult, ALU.add, [lgt.t], [lgt.t])
            kkT = P.sb("kkT", [128, 8, 512], BF16)
            vv = P.sb("vv", [128, 4, 1024], BF16)

            checkpoint('S0')
            rot_pool[0] = rot + accb
            with scope():
                cR = P.sb("cR", [128, 32 * 64], F32)
                sR = P.sb("sR", [128, 32 * 64], F32)
                aC = P.sb("aC", [128, 32 * 64], F32)
                aS = P.sb("aS", [128, 32 * 64], F32)
                for tb_, src_ in ((cR, cosR), (sR, sinR), (aC, axc), (aS, axs)):
                    ld(tb_[:, :].rearrange("p (g w) -> p g w", g=32), src_[:, :].rearrange("(g p) w -> p g w", p=128), [tb_.t])
                qg = P.sb("qg", [128, 128], F32)
                kg = P.sb("kg", [128, 128], F32)
                ld(qg[:, :], qn_in[l:l + 1, :].partition_broadcast(128), [qg.t])
                ld(kg[:, :], kn_in[l:l + 1, :].partition_broadcast(128), [kg.t])
                checkpoint('A0')
                xt = [P.sb(f"xtA{i}", [128, D], F32) for i in range(2)]
                junk = P.sb("junkA", [128, D], F32)
                hb = P.sb("hbA", [128, D], BF16)
                ss = P.sb("ssA", [128, 8], F32)
                hT = [P.sb(f"hTA{i}", [128, 8, 512], BF16) for i in range(2)]
                ring = [P.sb(f"wrA{i}", [128, 8, 512], BF16) for i in range(3)]
                ridx = [0]
                obf = [P.sb(f"obA{i}", [128, 512], BF16) for i in range(3)]
                o32 = [P.sb(f"o32A{i}", [128, 512], F32) for i in range(2)]
                xn32 = P.sb("xn32", [128, 512], F32)
                t1 = P.sb("t1A", [128, 256], F32)
                t2 = P.sb("t2A", [128, 256], F32)
                oi = [0]
                for T in range(DBG.get('T', 8)):
                    h_ = hT[T % 2]
                    for s in range(4):
                        xb_ = xt[s % 2]
                        ld(xb_[:, :], xsrc[T * 512 + s * 128:T * 512 + (s + 1) * 128, :], [xb_.t])
                        norm_T(xb_[:, :], [xb_.t], gp[:, 0:8], h_, s * 128, junk, hb, ss)
                    checkpoint('A1')
                    st(hTd[:, :, T * 512:(T + 1) * 512].rearrange("kc p t -> p kc t"), h_[:, :, :], [h_.t])
                    checkpoint('A2')
                    for grp in DBG.get('G', range(9)):
                        wb = wload(ring, ridx, wbf["w_in"][l], 0, 8, grp * 512, 512)
                        if grp < 3 or grp == 6:
                            for f in range(4):
                                ps = rotp()
                                for kc in range(8):
                                    P.mm(ps[:, :], wb[:, kc, f * 128:(f + 1) * 128], h_[:, kc, :], kc == 0, kc == 7, [wb.t, h_.t], [ps.t])
                                oi[0] = (oi[0] + 1) % 3
                                ob = obf[oi[0]]
                                evac(ob[:, :], ps[:, :], [ps.t], [ob.t])
                                r0 = grp * 512 + f * 128
                                if grp == 6:
                                    st(rgT[f * 128:(f + 1) * 128, T * 512:(T + 1) * 512], ob[:, :], [ob.t])
                                else:
                                    st(hyTd[r0:r0 + 128, T * 512:(T + 1) * 512], ob[:, :], [ob.t])
                        else:
                            for s in range(4):
                                g_ = T * 4 + s
                                tsl = slice(g_ * 128, (g_ + 1) * 128)
                                ps = rotp()
                                for kc in range(8):
                                    P.mm(ps[:, :], h_[:, kc, s * 128:(s + 1) * 128], wb[:, kc, :], kc == 0, kc == 7, [wb.t, h_.t], [ps.t])
                                oi[0] = (oi[0] + 1) % 3
                                ob = obf[oi[0]]
                                if grp == 5:
                                    evac(ob[:, :], ps[:, :], [ps.t], [ob.t])
                                    st(rvd[tsl, :], ob[:, :], [ob.t])
                                    continue
                                o3 = o32[s % 2]
                                cpa(o3[:, :], ps[:, :], [ps.t], [o3.t])
                                if grp in (3, 4):
                                    rotary(o3, 4, 1, 64, cR, sR, g_ * 64, 32 * 64, ob[:, :], ob.t, t1, t2)
                                    st((rqd if grp == 3 else rkd)[tsl, :], ob[:, :], [ob.t])
                                else:
                                    nh = 4 if grp == 7 else 2
                                    gt = qg if grp == 7 else kg
                                    mset(ss[:, 0:4], 0.0, [ss.t])
                                    for h in range(nh):
                                        act(junk[:, 0:128], o3[:, h * 128:(h + 1) * 128], AF.Square, [o3.t], [junk.t, ss.t], accum_out=ss[:, h:h + 1])
                                    rstd_from_ss(ss, nh, 1.0 / 128, [ss.t])
                                    for h in range(nh):
                                        stt(xn32[:, h * 128:(h + 1) * 128], o3[:, h * 128:(h + 1) * 128], ss[:, h:h + 1], gt[:, :], ALU.mult, ALU.mult,
                                            [o3.t, ss.t, gt.t], [xn32.t])
                                    rotary(xn32, nh, 2, 32, aC, aS, g_ * 64, 32 * 64, ob[:, 0:nh * 128], ob.t, t1, t2)
                                    if grp == 7:
                                        st(aqd[tsl, :], ob[:, :], [ob.t])
                                    else:
                                        cpv(ob[:, 256:512], o3[:, 256:512], [o3.t], [ob.t])
                                        st(akd[tsl, :], ob[:, 0:256], [ob.t])
                                        st(avd[tsl, :], ob[:, 256:512], [ob.t])
            rot_pool[0] = list(rot)
            checkpoint('A')
            with scope():
                w1s = P.sb("w1s", [33, 64], F32)
                w2s = P.sb("w2s", [64, 64], F32)
                w3s = P.sb("w3s", [64, 1024], F32)
                b1s = P.sb("b1s", [64, depth], F32)
                b2s = P.sb("b2s", [64, depth], F32)
                h2T = P.sb("h2T", [64, NT], F32)
                cw = P.sb("cw", [128, 36], F32)
                cwf = P.sb("cwf", [128, 36], F32)
                hbs = P.sb("hbs", [128, 4], F32)
                tcol = P.sb("tcol", [128, 32], F32)
                mkb = P.sb("mkb", [128, 32], F32)
                dlt = P.sb("dlt", [128, 512], F32)
                ld(w1s[:, :], fw1[l], [w1s.t]); ld(w2s[:, :], fw2[l], [w2s.t])
                ld(w3s[:, :], fw3[l], [w3s.t]); ld(b1s[:, :], fb1[:, :], [b1s.t]); ld(b2s[:, :], fb2[:, :], [b2s.t])
                ld(cw[:, :], convw[:, l * 36:(l + 1) * 36], [cw.t]); ld(hbs[:, :], hbias[:, l * 4:(l + 1) * 4], [hbs.t])
                ld(tcol[:, :], tcol_in[:, :], [tcol.t]); ld(mkb[:, :], maskb_in[:, :], [mkb.t])
                ld(dlt[:, :], delta_in[0:1, :].partition_broadcast(128), [dlt.t])
                ts(tcol[:, :], tcol[:, :], -1.0, 0.0, ALU.mult, ALU.add, [tcol.t], [tcol.t])
                stt(cwf[:, :], cw[:, :], mcoef, cw[:, :], ALU.mult, ALU.subtract, [cw.t, misc.t], [cwf.t])

                swt_ref = [None]

                def sin_layer(dst, wmat, kdim, src, bcol, btok):
                    for jb in range(8):
                        ps = rotp()
                        P.mm(ps[0:64, :], wmat[0:kdim, :], src[0:kdim, jb * 512:(jb + 1) * 512], True, True, [wmat.t, src.t], [ps.t])
                        a_, m1_, m2_ = swt_ref[0]
                        ts(a_[:, :], ps[0:64, :], bcol, 0.0, ALU.add, ALU.add, [ps.t, btok], [a_.t])
                        ts(m1_[:, :], a_[:, :], PI, -2 * PI, ALU.is_gt, ALU.mult, [a_.t], [m1_.t])
                        ts(m2_[:, :], a_[:, :], -PI, 2 * PI, ALU.is_lt, ALU.mult, [a_.t], [m2_.t])
                        tt(a_[:, :], a_[:, :], m1_[:, :], ALU.add, [a_.t, m1_.t], [a_.t])
                        tt(a_[:, :], a_[:, :], m2_[:, :], ALU.add, [a_.t, m2_.t], [a_.t])
                        ts(a_[:, :], a_[:, :], -3.14159, 3.14159, ALU.max, ALU.min, [a_.t], [a_.t])
                        act(dst[:, jb * 512:(jb + 1) * 512], a_[:, :], AF.Sin, [a_.t], [dst.t])
                with scope():
                    zE = P.sb("zE", [33, NT], F32)
                    h1T = P.sb("h1T", [64, NT], F32)
                    swt = [P.sb(f"swt{i}", [64, 512], F32) for i in range(3)]
                    swt_ref[0] = swt
                    ld(zE[:, :], zemb[:, :], [zE.t])
                    sin_layer(h1T, w1s, 33, zE, b1s[:, l:l + 1], b1s.t)
                    sin_layer(h2T, w2s, 64, h1T, b2s[:, l:l + 1], b2s.t)

                checkpoint('H0')
                for cg in range(2):
                    with scope():
                        zkc = P.sb("zkc", [128, 32, 512], BF16)
                        zks = P.sb("zks", [128, 32, 512], BF16)
                        Pr = P.sb("Pr", [128, 32, 256], BF16)
                        Pi = P.sb("Pi", [128, 32, 256], BF16)
                        with scope():
                            raw = [P.sb(f"raw{i}", [128, NT], BF16) for i in range(3)]
                            tA = P.sb("tA", [128, NT], F32)
                            tB = P.sb("tB", [128, NT], F32)
                            zb = P.sb("zb", [128, NT], BF16)
                            for cc in range(2):
                                c4 = cg * 2 + cc
                                for k3 in range(3):
                                    r0 = k3 * 512 + c4 * 128
                                    ld(raw[k3][:, :], hyTd[r0:r0 + 128, :], [raw[k3].t])

                                def conv(dst, src, k3):
                                    wc = lambda j: cw[:, j * 12 + k3 * 4 + c4:j * 12 + k3 * 4 + c4 + 1]
                                    wf = lambda j: cwf[:, j * 12 + k3 * 4 + c4:j * 12 + k3 * 4 + c4 + 1]
                                    R_ = [src.t, cw.t, cwf.t]
                                    ts(dst[:, :], src[:, :], wc(1), 0.0, ALU.mult, ALU.add, R_, [dst.t])
                                    stt(dst[:, 1:NT], src[:, 0:NT - 1], wc(0), dst[:, 1:NT], ALU.mult, ALU.add, R_ + [dst.t], [dst.t])
                                    stt(dst[:, 0:NT - 1], src[:, 1:NT], wc(2), dst[:, 0:NT - 1], ALU.mult, ALU.add, R_ + [dst.t], [dst.t])
                                    stt(dst[:, 2048:2049], src[:, 2047:2048], wf(0), dst[:, 2048:2049], ALU.mult, ALU.add, R_ + [dst.t], [dst.t])
                                    stt(dst[:, 2047:2048], src[:, 2048:2049], wf(2), dst[:, 2047:2048], ALU.mult, ALU.add, R_ + [dst.t], [dst.t])
                                conv(tA, raw[0], 0)
                                cpa(zb[:, :], tA[:, :], [tA.t], [zb.t])
                                st(x0cd[c4 * 128:(c4 + 1) * 128, :], zb[:, :], [zb.t])
                                conv(tA, raw[1], 1)
                                conv(tB, raw[2], 2)
                                tt(zb[:, :], tA[:, :], tB[:, :], ALU.mult, [tA.t, tB.t], [zb.t])
                                st(zTd[c4 * 128:(c4 + 1) * 128, :], zb[:, :], [zb.t])
                                for t4 in range(8):
                                    srcs = [zb[:, (t4 * 4 + i) * 128:(t4 * 4 + i + 1) * 128] for i in range(4)]
                                    dst = zkc[:, t4 * 4:(t4 + 1) * 4, cc * 128:(cc + 1) * 128]
                                    dst2 = zks[:, t4 * 4:(t4 + 1) * 4, cc * 128:(cc + 1) * 128]
                                    ph, ptok = tr_half()
                                    for i, s_ in enumerate(srcs):
                                        op("pe", lambda e, s_=s_, i=i, ph=ph: e.matmul(ph[:, i * 128:(i + 1) * 128], s_, ident[:, :], start=True, stop=True), [zb.t, ident.t], [ptok], sig=(i == 3))
                                    cpa(dst, ph.rearrange("p (a b) -> p a b", a=4), [ptok], [zkc.t])
                                    cpa(dst2, ph.rearrange("p (a b) -> p a b", a=4), [ptok], [zks.t])
                        checkpoint('H1')
                        with scope():
                            modt = [P.sb(f"modt{i}", [128, 256], F32) for i in range(2)]
                            bm = [P.sb(f"bm{i}", [128, 256], F32) for i in range(2)]
                            sdt = [P.sb(f"sdt{i}", [128, 256], F32) for i in range(2)]
                            for jc in range(32):
                                ps = rotp()
                                lh = h2T[:, jc * 128:(jc + 1) * 128]
                                P.mm(ps[:, 0:256], lh, w3s[:, cg * 256:(cg + 1) * 256], True, True, [h2T.t, w3s.t], [ps.t], sig=False)
                                P.mm(ps[:, 256:512], lh, w3s[:, 512 + cg * 256:512 + (cg + 1) * 256], True, True, [h2T.t, w3s.t], [ps.t])
                                md, b_, s_ = modt[jc % 2], bm[jc % 2], sdt[jc % 2]
                                act(md[:, :], dlt[:, cg * 256:(cg + 1) * 256], AF.Exp, [dlt.t, tcol.t], [md.t], scale=tcol[:, jc:jc + 1])
                                ts(b_[:, :], ps[:, 256:512], mkb[:, jc:jc + 1], 0.0, ALU.mult, ALU.add, [ps.t, mkb.t], [b_.t])
                                tt(s_[:, :], ps[:, 0:256], b_[:, :], ALU.add, [ps.t, b_.t], [s_.t])
                                tt(zkc[:, jc, 256:512], s_[:, :], md[:, :], ALU.mult, [s_.t, md.t], [zkc.t])
                                tt(s_[:, :], ps[:, 0:256], b_[:, :], ALU.subtract, [ps.t, b_.t], [s_.t])
                                tt(zks[:, jc, 256:512], s_[:, :], md[:, :], ALU.mult, [s_.t, md.t], [zks.t])
                        checkpoint('H2')
                        with scope():
                            cst = [P.sb(f"cst{i}", [128, 4096], BF16) for i in range(3)]
                            sst = [P.sb(f"sst{i}", [128, 4096], BF16) for i in range(3)]
                            A32 = P.sb("A32", [128, 512], F32)
                            B32 = P.sb("B32", [128, 512], F32)
                            q1 = P.sb("q1", [128, 256], F32)
                            q2 = P.sb("q2", [128, 256], F32)
                            for kc in range(32):
                                c_, s_ = cst[kc % 3], sst[kc % 3]
                                ld(c_[:, :], Cf[kc], [c_.t]); ld(s_[:, :], Sf[kc], [s_.t])
                                pC, pS = rotp(), rotp()
                                for tc in range(32):
                                    f, la = tc == 0, tc == 31
                                    P.mm(pC[:, :], c_[:, tc * 128:(tc + 1) * 128], zkc[:, tc, :], f, la, [c_.t, zkc.t], [pC.t])
                                    P.mm(pS[:, :], s_[:, tc * 128:(tc + 1) * 128], zks[:, tc, :], f, la, [s_.t, zks.t], [pS.t])
                                cpa(A32[:, :], pC[:, :], [pC.t], [A32.t])
                                cpa(B32[:, :], pS[:, :], [pS.t], [B32.t])
                                tt(q1[:, :], A32[:, 0:256], A32[:, 256:512], ALU.mult, [A32.t], [q1.t])
                                tt(q2[:, :], B32[:, 0:256], B32[:, 256:512], ALU.mult, [B32.t], [q2.t])
                                tt(Pr[:, kc, :], q1[:, :], q2[:, :], ALU.subtract, [q1.t, q2.t], [Pr.t])
                                tt(q1[:, :], A32[:, 0:256], B32[:, 256:512], ALU.mult, [A32.t, B32.t], [q1.t])
                                tt(q2[:, :], B32[:, 0:256], A32[:, 256:512], ALU.mult, [A32.t, B32.t], [q2.t])
                                tt(Pi[:, kc, :], q1[:, :], q2[:, :], ALU.add, [q1.t, q2.t], [Pi.t])
                        checkpoint('H3')
                        with scope():
                            cip = [P.sb(f"cip{i}", [128, 4096], BF16) for i in range(3)]
                            sip = [P.sb(f"sip{i}", [128, 4096], BF16) for i in range(3)]
                            x0t = [P.sb(f"x0t{i}", [128, 512], BF16) for i in range(2)]
                            zt_ = [P.sb(f"zt_{i}", [128, 512], BF16) for i in range(2)]
                            e1 = [P.sb(f"e1{i}", [128, 512], F32) for i in range(2)]
                            yo = [P.sb(f"yo{i}", [128, 512], BF16) for i in range(2)]
                            pi_ = [0]
                            for tb in range(8):
                                pacc = [accb[0], accb[1]]
                                for piece in range(4):
                                    pi_[0] = (pi_[0] + 1) % 3
                                    c_, s_ = cip[pi_[0]], sip[pi_[0]]
                                    ld(c_[:, :], Ci[tb, piece], [c_.t]); ld(s_[:, :], Si[tb, piece], [s_.t])
                                    for kk in range(8):
                                        kc = piece * 8 + kk
                                        for cc in range(2):
                                            P.mm(pacc[cc][:, :], Pr[:, kc, cc * 128:(cc + 1) * 128], c_[:, kk * 512:(kk + 1) * 512], kc == 0, False, [Pr.t, c_.t], [pacc[cc].t], sig=False)
                                            P.mm(pacc[cc][:, :], Pi[:, kc, cc * 128:(cc + 1) * 128], s_[:, kk * 512:(kk + 1) * 512], False, kc == 31, [Pi.t, s_.t], [pacc[cc].t], sig=(kc == 31 or (kk == 7 and cc == 1)))
                                for cc in range(2):
                                    c4 = cg * 2 + cc
                                    rows = slice(c4 * 128, (c4 + 1) * 128)
                                    cols = slice(tb * 512, (tb + 1) * 512)
                                    ld(x0t[cc][:, :], x0cd[rows, cols], [x0t[cc].t]); ld(zt_[cc][:, :], zTd[rows, cols], [zt_[cc].t])
                                    ts(e1[cc][:, :], zt_[cc][:, :], hbs[:, c4:c4 + 1], 0.0, ALU.mult, ALU.add, [zt_[cc].t, hbs.t], [e1[cc].t])
                                    tt(e1[cc][:, :], pacc[cc][:, :], e1[cc][:, :], ALU.add, [pacc[cc].t, e1[cc].t], [e1[cc].t])
                                    tt(yo[cc][:, :], e1[cc][:, :], x0t[cc][:, :], ALU.mult, [e1[cc].t, x0t[cc].t], [yo[cc].t])
                                    st(yT3[rows, cols], yo[cc][:, :], [yo[cc].t])

            checkpoint('H')
            for mode in range(2):
                if mode == 1:
                    checkpoint('AT')
                with scope():
                    nkv = 2 if mode == 0 else 4
                    ksrc, vsrc, qsrc = (akd, avd, aqd) if mode == 0 else (rkd, rvd, rqd)
                    kT = P.sb("kT", [128, nkv, NT], BF16)
                    V = P.sb("V", [128, 32, nkv * 128], BF16)
                    ktl = [P.sb(f"ktl{i}", [128, nkv * 128], BF16) for i in range(2)]
                    qtl = [P.sb(f"qtl{i}", [128, 512], BF16) for i in range(2)]
                    qT = P.sb("qT", [128, 4, 512], BF16)
                    PT = [P.sb(f"PT{i}", [128, 512], BF16) for i in range(3)]
                    Dt = [P.sb(f"Dt{i}", [128, 512], F32) for i in range(2)]
                    fin = [P.sb(f"fin{i}", [128, 512], F32) for i in range(4)]
                    yo = [P.sb(f"yoA{i}", [128, 512], BF16) for i in range(2)]
                    rgt = P.sb("rgt", [128, 512], BF16)
                    bF = P.sb("bF", [128, 4, 256], F32)
                    bB = P.sb("bB", [128, 4, 256], F32)
                    djq = P.sb("djq", [128, 256], F32)
                    mjq = P.sb("mjq", [128, 256], F32)
                    base = P.sb("base", [128, 512], F32)
                    nlg = P.sb("nlg", [128, 8], F32)
                    o32f = P.sb("o32f", [128, 128], F32)
                    ld(V[:, :, :], vsrc[:, :].rearrange("(g p) c -> p g c", p=128), [V.t])
                    ld(djq[:, :], djq_in[:, :], [djq.t]); ld(mjq[:, :], mjq_all[:, mode * 256:(mode + 1) * 256], [mjq.t]); ld(base[:, :], base_in[:, :], [base.t])
                    mset(o32f[:, :], 1.0 / 128, [o32f.t])
                    ts(nlg[:, :], lgt[:, :], -1.0, 0.0, ALU.mult, ALU.add, [lgt.t], [nlg.t])
                    if mode == 1:
                        for h in range(4):
                            stt(bF[:, h, :], djq[:, :], lgt[:, h:h + 1], mjq[:, :], ALU.mult, ALU.add, [djq.t, lgt.t, mjq.t], [bF.t])
                            stt(bB[:, h, :], djq[:, :], nlg[:, 4 + h:5 + h], mjq[:, :], ALU.mult, ALU.add, [djq.t, nlg.t, mjq.t], [bB.t])
                        Dmix = P.sb("Dmix", [128, 16, 512], F32)
                        for h in range(4):
                            for di in range(4):
                                f0, f1 = fin[0], fin[1]
                                ts(f0[:, :], base[:, :], float(-128 * di), 0.0, ALU.add, ALU.max, [base.t], [f0.t])
                                ts(f1[:, :], base[:, :], float(-128 * di), 0.0, ALU.add, ALU.min, [base.t], [f1.t])
                                ts(f0[:, :], f0[:, :], lgt[:, h:h + 1], 0.0, ALU.mult, ALU.add, [f0.t, lgt.t], [f0.t])
                                stt(f0[:, :], f1[:, :], nlg[:, 4 + h:5 + h], f0[:, :], ALU.mult, ALU.add, [f1.t, nlg.t, f0.t], [f0.t])
                                act(Dmix[:, h * 4 + di, :], f0[:, :], AF.Exp, [f0.t, mjq.t], [Dmix.t], bias=mjq[:, 0:1])
                    for g in range(32):
                        kt_ = ktl[g % 2]
                        ld(kt_[:, :], ksrc[g * 128:(g + 1) * 128, :], [kt_.t])
                        ph, ptok = tr_half()
                        for h in range(nkv):
                            op("pe", lambda e, h=h, ph=ph, kt_=kt_: e.matmul(ph[:, h * 128:(h + 1) * 128], kt_[:, h * 128:(h + 1) * 128], ident[:, :], start=True, stop=True), [kt_.t, ident.t], [ptok], sig=(h == nkv - 1))
                        cpa(kT[:, :, g * 128:(g + 1) * 128], ph[:, 0:nkv * 128].rearrange("p (a b) -> p a b", a=nkv), [ptok], [kT.t])
                    pti = [0]
                    for qb in range(8):
                        for s in range(4):
                            qt_ = qtl[s % 2]
                            ld(qt_[:, :], qsrc[qb * 512 + s * 128:qb * 512 + (s + 1) * 128, :], [qt_.t])
                            ph, ptok = tr_half()
                            for h in range(4):
                                op("pe", lambda e, h=h, ph=ph, qt_=qt_: e.matmul(ph[:, h * 128:(h + 1) * 128], qt_[:, h * 128:(h + 1) * 128], ident[:, :], start=True, stop=True), [qt_.t, ident.t], [ptok], sig=(h == 3))
                            cpa(qT[:, :, s * 128:(s + 1) * 128], ph.rearrange("p (a b) -> p a b", a=4), [ptok], [qT.t])
                        for h in range(4):
                            kvh = h // 2 if mode == 0 else h
                            O, Dn = accb[0], accb[1]
                            for j in range(32):
                                ps = rotp()
                                P.mm(ps[:, :], kT[:, kvh, j * 128:(j + 1) * 128], qT[:, h, :], True, True, [kT.t, qT.t], [ps.t])
                                pti[0] = (pti[0] + 1) % 3
                                pt = PT[pti[0]]
                                jq = j * 8 + qb
                                if mode == 0:
                                    act(pt[:, :], ps[:, :], AF.Exp, [ps.t, mjq.t], [pt.t], scale=SC128, bias=mjq[:, jq:jq + 1])
                                else:
                                    dlt_ = 512 * qb - 128 * j
                                    dt__ = Dt[j % 2]
                                    if dlt_ >= 128:
                                        act(dt__[:, :], base[:, :], AF.Exp, [base.t, bF.t, lgt.t], [dt__.t], scale=lgt[:, h:h + 1], bias=bF[:, h, jq:jq + 1])
                                    elif dlt_ <= -512:
                                        act(dt__[:, :], base[:, :], AF.Exp, [base.t, bB.t, nlg.t], [dt__.t], scale=nlg[:, 4 + h:5 + h], bias=bB[:, h, jq:jq + 1])
                                    else:
                                        dmx = Dmix[:, h * 4 + (-dlt_) // 128, :]
                                    if -512 < dlt_ < 128:
                                        tt(pt[:, :], ps[:, :], dmx, ALU.mult, [ps.t, Dmix.t], [pt.t])
                                    else:
                                        tt(pt[:, :], ps[:, :], dt__[:, :], ALU.mult, [ps.t, dt__.t], [pt.t])
                                P.mm(O[:, :], V[:, j, kvh * 128:(kvh + 1) * 128], pt[:, :], j == 0, j == 31, [V.t, pt.t], [O.t], sig=(j == 31))
                                if mode == 0:
                                    P.mm(Dn[:, :], ones[:, :], pt[:, :], j == 0, j == 31, [ones.t, pt.t], [Dn.t], sig=(j == 31))
                            y_ = yo[h % 2]
                            cols = slice(qb * 512, (qb + 1) * 512)
                            if mode == 0:
                                recip(fin[0][:, :], Dn[:, :], [Dn.t], [fin[0].t])
                                tt(y_[:, :], O[:, :], fin[0][:, :], ALU.mult, [O.t, fin[0].t], [y_.t])
                                st(yT3[1024 + h * 128:1024 + (h + 1) * 128, cols], y_[:, :], [y_.t])
                            else:
                                o_, sq_, g_, g2_ = fin
                                cpa(o_[:, :], O[:, :], [O.t], [o_.t])
                                tt(sq_[:, :], o_[:, :], o_[:, :], ALU.mult, [o_.t], [sq_.t])
                                m1, m2 = rotp(), rotp()
                                P.mm(m1[:, :], o32f[:, :], o_[:, :], True, True, [o32f.t, o_.t], [m1.t])
                                P.mm(m2[:, :], o32f[:, :], sq_[:, :], True, True, [o32f.t, sq_.t], [m2.t])
                                tt(o_[:, :], o_[:, :], m1[:, :], ALU.subtract, [o_.t, m1.t], [o_.t])
                                act(sq_[:, :], m1[:, :], AF.Square, [m1.t], [sq_.t])
                                tt(sq_[:, :], m2[:, :], sq_[:, :], ALU.subtract, [m2.t, sq_.t], [sq_.t])
                                ts(sq_[:, :], sq_[:, :], EPS, 0.0, ALU.add, ALU.add, [sq_.t], [sq_.t])
                                act(sq_[:, :], sq_[:, :], AF.Ln, [sq_.t], [sq_.t])
                                act(sq_[:, :], sq_[:, :], AF.Exp, [sq_.t], [sq_.t], scale=-0.5)
                                tt(o_[:, :], o_[:, :], sq_[:, :], ALU.mult, [o_.t, sq_.t], [o_.t])
                                ld(rgt[:, :], rgT[h * 128:(h + 1) * 128, cols], [rgt.t])
                                act(g_[:, :], rgt[:, :], AF.Tanh, [rgt.t], [g_.t], scale=0.5)
                                ts(g_[:, :], g_[:, :], 0.5, 0.5, ALU.mult, ALU.add, [g_.t], [g_.t])
                                tt(g_[:, :], g_[:, :], rgt[:, :], ALU.mult, [g_.t, rgt.t], [g_.t])
                                tt(y_[:, :], o_[:, :], g_[:, :], ALU.mult, [o_.t, g_.t], [y_.t])
                                st(yT3[512 + h * 128:512 + (h + 1) * 128, cols], y_[:, :], [y_.t])
            checkpoint('RT')
            with scope():
                mt = [P.sb(f"mt{i}", [128, D], F32) for i in range(2)]
                junk = P.sb("junkM", [128, D], F32)
                hb = P.sb("hbM", [128, D], BF16)
                ss = P.sb("ssM", [128, 8], F32)
                memT = P.sb("memT", [128, 8, 512], BF16)
                ring = [P.sb(f"wrM{i}", [128, 8, 512], BF16) for i in range(2)]
                ridx = [0]
                for sl in range(2):
                    for mc in range(2):
                        m_ = mt[mc]
                        ld(m_[:, :], mem_in[sl, mc * 128:(mc + 1) * 128, :], [m_.t])
                        norm_T(m_[:, :], [m_.t], gp[:, 24:32], memT, sl * 256 + mc * 128, junk, hb, ss)
                for grp in range(4):
                    wb = wload(ring, ridx, wbf["w_xkv"][l], 0, 8, grp * 512, 512)
                    if grp < 2:
                        for f in range(4):
                            ps = rotp()
                            for kc in range(8):
                                P.mm(ps[:, :], wb[:, kc, f * 128:(f + 1) * 128], memT[:, kc, :], kc == 0, kc == 7, [wb.t, memT.t], [ps.t])
                            evac(kkT[:, grp * 4 + f, :], ps[:, :], [ps.t], [kkT.t])
                    else:
                        for sm in range(4):
                            ps = rotp()
                            for kc in range(8):
                                P.mm(ps[:, :], memT[:, kc, sm * 128:(sm + 1) * 128], wb[:, kc, :], kc == 0, kc == 7, [wb.t, memT.t], [ps.t])
                            evac(vv[:, sm, (grp - 2) * 512:(grp - 1) * 512], ps[:, :], [ps.t], [vv.t])

            rot_pool[0] = rot + [accb[0], accb[1]]
            with scope():
                gpo = [P.sb(f"gpo{i}", [128, D], F32) for i in range(3)]
                for i in range(3):
                    ld(gpo[i][:, :], gpost[l * 3 + i:l * 3 + i + 1, :].partition_broadcast(128), [gpo[i].t])
                xt = P.sb("xtE", [128, 4, D], F32)
                fmA = P.sb("fmA", [128, 8, 512], BF16)
                fmB = P.sb("fmB", [128, 8, 512], BF16)
                fmC = P.sb("fmC", [128, 8, 512], BF16)
                yTall = P.sb("yTall", [128, 12, 512], BF16)
                macc4 = P.sb("macc4", [128, 4, 512], F32)
                gt_ = [P.sb(f"gtE{i}", [128, 512], F32) for i in range(2)]
                osb = P.sb("osb", [128, 4, D], F32)
                junk = P.sb("junkE", [128, D], F32)
                hb = P.sb("hbE", [128, D], BF16)
                ss = P.sb("ssE", [128, 8], F32)
                PTx = [P.sb(f"PTx{i}", [128, 2, 512], BF16) for i in range(2)]
                uT = P.sb("uT", [128, 32, 512], BF16)
                rl = [P.sb(f"rl{i}", [128, 512], BF16) for i in range(2)]
                ring = [P.sb(f"wrE{i}", [128, 8, 512], BF16) for i in range(4)]
                ridx = [0]

                def post_norm_residual(lhs_fm, nfc, wsrc, gtile, T):
                    for ch in range(2):
                        for f0 in range(0, nfc, 8):
                            wb = wload(ring, ridx, wsrc, f0 * 128, 8, ch * 512, 512)
                            for s in range(4):
                                ps = quad[s]
                                for k_ in range(8):
                                    fc = f0 + k_
                                    P.mm(ps[:, :], lhs_fm[:, fc, s * 128:(s + 1) * 128], wb[:, k_, :], fc == 0, fc == nfc - 1, [lhs_fm.t, wb.t], [ps.t], sig=(fc == nfc - 1 or (k_ == 7 and s == 3)))
                        for s in range(4):
                            evac(osb[:, s, ch * 512:(ch + 1) * 512], quad[s][:, :], [quad[s].t], [osb.t])
                    for s in range(4):
                        mset(ss[:, 0:1], 0.0, [ss.t])
                        act(junk[:, :], osb[:, s, :], AF.Square, [osb.t], [junk.t, ss.t], accum_out=ss[:, 0:1])
                        rstd_from_ss(ss, 1, 1.0 / D, [ss.t])
                        tt(osb[:, s, :], osb[:, s, :], gtile[:, :], ALU.mult, [osb.t, gtile.t], [osb.t])
                        stt(xt[:, s, :], osb[:, s, :], ss[:, 0:1], xt[:, s, :], ALU.mult, ALU.add, [osb.t, ss.t, xt.t], [xt.t])

                wcache = {}
                for T in range(8):
                    sl = T // 4
                    for s in range(4):
                        ld(xt[:, s, :], xsrc[T * 512 + s * 128:T * 512 + (s + 1) * 128, :], [xt.t])
                    ld(fmA[:, :, :], hTd[:, :, T * 512:(T + 1) * 512].rearrange("kc p t -> p kc t"), [fmA.t])
                    ld(yTall[:, :, :], yT3[:, T * 512:(T + 1) * 512].rearrange("(fc p) t -> p fc t", p=128), [yTall.t])
                    for dc4 in range(2):
                        for br in range(3):
                            wg = wload(ring, ridx, wbf["w_in"][l], 0, 8, 4608 + br * 1024 + dc4 * 512, 512)
                            wbr = wload(ring, ridx, wbf["w_br"][l], br * 512, 4, dc4 * 512, 512)
                            for dl in range(4):
                                dc = dc4 * 4 + dl
                                pg = rotp()
                                for kc in range(8):
                                    P.mm(pg[:, :], wg[:, kc, dl * 128:(dl + 1) * 128], fmA[:, kc, :], kc == 0, kc == 7, [wg.t, fmA.t], [pg.t])
                                pp = rotp()
                                for fc in range(4):
                                    P.mm(pp[:, :], wbr[:, fc, dl * 128:(dl + 1) * 128], yTall[:, br * 4 + fc, :], fc == 0, fc == 3, [wbr.t, yTall.t], [pp.t])
                                g_ = gt_[dl % 2]
                                act(g_[:, :], pg[:, :], AF.Tanh, [pg.t], [g_.t], scale=0.5)
                                ts(g_[:, :], g_[:, :], 0.5, 0.5, ALU.mult, ALU.add, [g_.t], [g_.t])
                                if br == 0:
                                    tt(macc4[:, dl, :], pp[:, :], g_[:, :], ALU.mult, [pp.t, g_.t], [macc4.t])
                                else:
                                    tt(g_[:, :], pp[:, :], g_[:, :], ALU.mult, [pp.t, g_.t], [g_.t])
                                    if br == 1:
                                        tt(macc4[:, dl, :], macc4[:, dl, :], g_[:, :], ALU.add, [macc4.t, g_.t], [macc4.t])
                                    else:
                                        tt(fmB[:, dc, :], macc4[:, dl, :], g_[:, :], ALU.add, [macc4.t, g_.t], [fmB.t])
                    post_norm_residual(fmB, 8, wbf["w_out"][l], gpo[0], T)
                    for s in range(4):
                        norm_T(xt[:, s, :], [xt.t], gp[:, 8:16], fmA, s * 128, junk, hb, ss)
                    for fg in range(2):
                        wb = wload(ring, ridx, wbf["w_xq"][l], 0, 8, fg * 512, 512)
                        for f in range(4):
                            ps = rotp()
                            for kc in range(8):
                                P.mm(ps[:, :], wb[:, kc, f * 128:(f + 1) * 128], fmA[:, kc, :], kc == 0, kc == 7, [wb.t, fmA.t], [ps.t])
                            evac(fmB[:, fg * 4 + f, :], ps[:, :], [ps.t], [fmB.t])
                    for h in range(4):
                        ptx = PTx[h % 2]
                        for mc in range(2):
                            ps = rotp()
                            for c2 in range(2):
                                P.mm(ps[:, :], kkT[:, 2 * h + c2, sl * 256 + mc * 128:sl * 256 + (mc + 1) * 128], fmB[:, 2 * h + c2, :], c2 == 0, c2 == 1, [kkT.t, fmB.t], [ps.t])
                            act(ptx[:, mc, :], ps[:, :], AF.Exp, [ps.t], [ptx.t], scale=1.0 / 16.0)
                        pd = rotp()
                        for mc in range(2):
                            P.mm(pd[:, :], ones[:, :], ptx[:, mc, :], mc == 0, mc == 1, [ones.t, ptx.t], [pd.t])
                        g_ = gt_[h % 2]
                        recip(g_[:, :], pd[:, :], [pd.t], [g_.t])
                        for c2 in range(2):
                            po = rotp()
                            for mc in range(2):
                                P.mm(po[:, :], vv[:, sl * 2 + mc, (2 * h + c2) * 128:(2 * h + c2 + 1) * 128], ptx[:, mc, :], mc == 0, mc == 1, [vv.t, ptx.t], [po.t])
                            tt(fmC[:, 2 * h + c2, :], po[:, :], g_[:, :], ALU.mult, [po.t, g_.t], [fmC.t])
                    post_norm_residual(fmC, 8, wbf["w_xo"][l], gpo[1], T)
                    for s in range(4):
                        norm_T(xt[:, s, :], [xt.t], gp[:, 16:24], fmA, s * 128, junk, hb, ss)
                    for fg in range(8):
                        wb = wload(ring, ridx, wbf["w_ff1"][l], 0, 8, fg * 512, 512)
                        for f in range(4):
                            ps = rotp()
                            for kc in range(8):
                                P.mm(ps[:, :], wb[:, kc, f * 128:(f + 1) * 128], fmA[:, kc, :], kc == 0, kc == 7, [wb.t, fmA.t], [ps.t])
                            r_ = rl[f % 2]
                            act(r_[:, :], ps[:, :], AF.Relu, [ps.t], [r_.t])
                            tt(uT[:, fg * 4 + f, :], r_[:, :], r_[:, :], ALU.mult, [r_.t], [uT.t])
                    post_norm_residual(uT, 32, wbf["w_ff2"][l], gpo[2], T)
                    for s in range(4):
                        st(xdst[T * 512 + s * 128:T * 512 + (s + 1) * 128, :], xt[:, s, :], [xt.t])
    except _Stop:
        pass
    P.finish()
    return nc


def _core_tables(L):
    import ml_dtypes
    nseq = NT // L
    tl = np.arange(NT) % L
    t = {}
    inv = 10000.0 ** (-np.arange(0, 128, 2, dtype=np.float32) / 128)
    ang = tl.astype(np.float32)[:, None] * inv[None, :]
    t["cosR"], t["sinR"] = np.cos(ang).astype(np.float32), np.sin(ang).astype(np.float32)
    inv2 = 10000.0 ** (-np.arange(0, 64, 2, dtype=np.float32) / 64)
    ar = (tl // 64).astype(np.float32)[:, None] * inv2[None, :]
    ac = (tl % 64).astype(np.float32)[:, None] * inv2[None, :]
    t["axc"] = np.concatenate([np.cos(ar), np.cos(ac)], 1).astype(np.float32)
    t["axs"] = np.concatenate([np.sin(ar), np.sin(ac)], 1).astype(np.float32)
    tt_ = np.linspace(0.0, 1.0, L, dtype=np.float32)[:, None]
    bands = 16
    w = (2.0 * math.pi * np.arange(L, dtype=np.float32)[:, None] / L).astype(np.float32)
    f = np.linspace(1e-4, bands - 1, bands, dtype=np.float32)[None, :]
    z = np.concatenate([tt_, np.cos(f * w), -np.sin(f * w)], -1).astype(np.float32)
    z = np.tile(z, (nseq, 1))
    t["zemb"] = np.ascontiguousarray(z.T)
    tj = np.tile(tt_[:, 0], nseq)
    t["tcol"] = np.ascontiguousarray(tj.reshape(32, 128).T)
    mk = np.ones(NT, np.float32); mk[tl == 0] = 0.0
    t["maskb"] = np.ascontiguousarray(mk.reshape(32, 128).T)
    min_decay = math.log(1e-2) / 1.5
    max_decay = math.log(1e-2) / 0.3
    t["delta"] = np.abs(np.linspace(min_decay, max_decay, 512, dtype=np.float32))[None, :].astype(np.float32)
    misc = np.zeros((128, 8), np.float32)
    misc[:, 0] = 1.0 if nseq == 1 else 0.0
    t["misc"] = misc
    jj, qq = np.meshgrid(np.arange(32), np.arange(8), indexing="ij")
    djq = (512 * qq - 128 * jj).astype(np.float32).reshape(-1)
    t["djq"] = np.tile(djq[None, :], (128, 1)).astype(np.float32)
    cross = ((jj * 128) // L != (qq * 512) // L).reshape(-1)
    mb = np.where(cross, -30000.0, 0.0).astype(np.float32)
    mjq = np.concatenate([mb, mb + math.log(SC128)])[None, :]
    t["mjq"] = np.tile(mjq, (128, 1)).astype(np.float32)
    t["base"] = (np.arange(512, dtype=np.float32)[None, :] - np.arange(128, dtype=np.float32)[:, None]).astype(np.float32)
    N = 2 * L
    k = np.arange(L, dtype=np.float64)
    n = np.arange(L, dtype=np.float64)
    th = np.pi * np.outer(n, 2 * k + 1) / N
    C1, S1 = np.cos(th), np.sin(th)
    Cfull = np.zeros((NT, NT), np.float32); Sfull = np.zeros((NT, NT), np.float32)
    for s_ in range(nseq):
        Cfull[s_ * L:(s_ + 1) * L, s_ * L:(s_ + 1) * L] = C1
        Sfull[s_ * L:(s_ + 1) * L, s_ * L:(s_ + 1) * L] = S1
    bf = ml_dtypes.bfloat16
    def fwd(M):
        return np.ascontiguousarray(M.reshape(32, 128, 32, 128).transpose(2, 1, 0, 3).reshape(32, 128, 4096)).astype(bf)
    t["Cf"], t["Sf"] = fwd(Cfull), fwd(Sfull)
    def invm(M):
        Mi = (M.T * (2.0 / N)).astype(np.float32)
        return np.ascontiguousarray(Mi.reshape(4, 8, 128, 8, 512).transpose(3, 0, 2, 1, 4).reshape(8, 4, 128, 4096)).astype(bf)
    t["Ci"], t["Si"] = invm(Cfull), invm(Sfull)
    return t


def _shared_tables():
    p = np.arange(128, dtype=np.float32)
    d = p[None, :] - p[:, None]
    ct = np.zeros((128, 6 * 128 + 8), np.float32)
    ct[:, 0:128] = np.maximum(d, 0); ct[:, 128:256] = np.maximum(-d, 0)
    ct[:, 256:384] = (d >= 0); ct[:, 384:512] = (d < 0)
    ct[:, 512:640] = p[None, :] + 1; ct[:, 640:768] = 128 - p[None, :]
    ct[:, 768] = p; ct[:, 769] = 127 - p; ct[:, 770] = 128.0
    return {"ctab": ct, "ident": np.eye(128, dtype=np.float32)}


_NC_CACHE = {}


def make_in_maps(inputs, depth, cores=range(8)):
    f32 = lambda a: np.ascontiguousarray(np.asarray(a, dtype=np.float32))
    inp = {k: f32(v) for k, v in inputs.items()}
    shared = {k: inp[k] for k in ("w_in", "w_branch", "w_out", "w_xq", "w_xkv", "w_xo", "w_ff1", "w_ff2",
                                  "hy_fw1", "hy_fw2", "hy_fw3", "att_qnorm", "att_knorm")}
    shared["convw"] = np.ascontiguousarray(inp["hy_conv"].reshape(depth, 3, 12, 128).transpose(3, 0, 1, 2).reshape(128, depth * 36))
    shared["fb1c"] = np.ascontiguousarray(inp["hy_fb1"].T)
    shared["fb2c"] = np.ascontiguousarray(inp["hy_fb2"].T)
    shared["hbiasc"] = np.ascontiguousarray(inp["hy_bias"].reshape(depth, 4, 128).transpose(2, 0, 1).reshape(128, depth * 4))
    shared["rdec"] = np.ascontiguousarray(inp["ret_decay"].reshape(depth, 8))
    gcols = lambda g: g.reshape(depth, 8, 128).transpose(2, 0, 1)
    shared["gpre"] = np.ascontiguousarray(np.concatenate([gcols(inp["g_mix_pre"]), gcols(inp["g_x_pre"]), gcols(inp["g_ff_pre"]), gcols(inp["g_mem"])], axis=2).reshape(128, depth * 32))
    shared["gpost"] = np.ascontiguousarray(np.stack([inp["g_mix_post"], inp["g_x_post"], inp["g_ff_post"]], axis=1).reshape(depth * 3, D))
    shared.update(_shared_tables())
    tabs = {4096: _core_tables(4096), 2048: _core_tables(2048)}
    in_maps = []
    for c in cores:
        m = dict(shared)
        if c < 4:
            m["x"] = inp["x_prompt"][c]
            m["mem"] = np.ascontiguousarray(np.stack([inp["mem_prompt"][c], inp["mem_prompt"][c]]))
            m.update(tabs[4096])
        else:
            b0 = 2 * (c - 4)
            m["x"] = np.ascontiguousarray(inp["x_sample"][b0:b0 + 2].reshape(NT, D))
            m["mem"] = np.ascontiguousarray(inp["mem_sample"][b0:b0 + 2])
            m.update(tabs[2048])
        in_maps.append(m)
    return in_maps


def kernel(**inputs):
    depth = L_DEPTH
    if "nc" not in _NC_CACHE:
        _NC_CACHE["nc"] = build_program(depth)
    nc = _NC_CACHE["nc"]
    in_maps = make_in_maps(inputs, depth)
    res = run_bass_kernel_spmd(nc, in_maps, core_ids=list(range(8)))
    outs = [np.asarray(r["y"], dtype=np.float32) for r in res.results]
    y_prompt = np.stack(outs[0:4]).reshape(4, 4096, D)
    y_sample = np.concatenate([o.reshape(2, 2048, D) for o in outs[4:8]], axis=0)
    return (y_prompt, y_sample)
```
